# Optimizing a Trainium2 kernel written in Bass

```python
import jax, jax.numpy as jnp
from jax import lax
import numpy as np

D_MODEL = 1024
BATCH = 8
SEQ = 4096
DEPTH = 2
DEC_BATCH = 16
DEC_SEQ = 64
PAST_LEN = 4096

CHUNK = 64
N_A_LAYERS = DEPTH // 2
N_B_LAYERS = DEPTH - N_A_LAYERS
A_HEADS = 4
A_DK = D_MODEL // 2 // A_HEADS
A_DV = D_MODEL // A_HEADS
A_QK = A_HEADS * A_DK
A_VW = A_HEADS * A_DV
A_GATE_RANK = 16
A_GATE_TAU = 16.0
A_IN = 2 * A_QK + 2 * A_VW + A_GATE_RANK
B_HEADS = 16
B_HEAD_DIM = D_MODEL // B_HEADS
Q_BLOCK = 128
D_FF = 2816
EPS = 1e-6
NEG_INF = -1e30

kernel_name = 'gla_fox_yoco_macaron_stream_step'


def rms_norm(x, g):
    xf = x.astype(jnp.float32)
    y = xf * lax.rsqrt(jnp.mean(xf * xf, axis=-1, keepdims=True) + EPS)
    return (y * g.astype(jnp.float32)).astype(x.dtype)


def swiglu_ffn(x, w_gu, w_down):
    g, u = jnp.split(x @ w_gu, 2, axis=-1)
    return (jax.nn.silu(g) * u) @ w_down


def gla_mixer(xn, w_in, w_g2, b_g, g_out, w_o, s0, chunk):
    bsz, t, _ = xn.shape
    n_c = t // chunk
    proj = xn @ w_in
    q = proj[..., :A_QK]
    k = proj[..., A_QK:2 * A_QK]
    v = proj[..., 2 * A_QK:2 * A_QK + A_VW]
    r = proj[..., 2 * A_QK + A_VW:2 * A_QK + 2 * A_VW]
    gl = proj[..., 2 * A_QK + 2 * A_VW:]
    log_a = jax.nn.log_sigmoid((gl @ w_g2 + b_g).astype(jnp.float32)) / A_GATE_TAU

    def to_chunks(z, d):
        return z.reshape(bsz, n_c, chunk, A_HEADS, d).transpose(0, 3, 1, 2, 4).astype(jnp.float32)

    qc = to_chunks(q, A_DK) * (A_DK ** -0.5)
    kc = to_chunks(k, A_DK)
    vc = to_chunks(v, A_DV)
    b = jnp.cumsum(to_chunks(log_a, A_DK), axis=3)
    b_last = b[:, :, :, -1:, :]
    q_e = qc * jnp.exp(b)
    k_e = kc * jnp.exp(-b)
    causal = jnp.tril(jnp.ones((chunk, chunk), dtype=bool))
    att = jnp.where(causal, jnp.einsum('bhcld,bhcmd->bhclm', q_e, k_e), 0.0)
    o_intra = jnp.einsum('bhclm,bhcme->bhcle', att, vc)
    decay_c = jnp.exp(b_last[:, :, :, 0, :])
    k_dec = kc * jnp.exp(b_last - b)

    def chunk_step(s, inp):
        dec, kd, vv, qq = inp
        o_inter = jnp.einsum('bhld,bhde->bhle', qq, s)
        s_new = dec[..., None] * s + jnp.einsum('bhld,bhle->bhde', kd, vv)
        return s_new, o_inter

    xs = (jnp.moveaxis(decay_c, 2, 0), jnp.moveaxis(k_dec, 2, 0), jnp.moveaxis(vc, 2, 0), jnp.moveaxis(q_e, 2, 0))
    s_fin, o_inter = lax.scan(chunk_step, s0.astype(jnp.float32), xs)
    o = o_intra + jnp.moveaxis(o_inter, 0, 2)
    o = o.transpose(0, 2, 3, 1, 4).reshape(bsz, t, A_HEADS, A_DV)
    o = rms_norm(o, g_out).astype(xn.dtype).reshape(bsz, t, A_VW) * jax.nn.silu(r)
    return o @ w_o, s_fin.astype(s0.dtype)


def shared_kv(h, kv_norm, w_kvf, b_f, g_k):
    bsz, t, _ = h.shape
    kvf = rms_norm(h, kv_norm) @ w_kvf
    k = rms_norm(kvf[..., :D_MODEL].reshape(bsz, t, B_HEADS, B_HEAD_DIM), g_k)
    v = kvf[..., D_MODEL:2 * D_MODEL].reshape(bsz, t, B_HEADS, B_HEAD_DIM)
    logf = jax.nn.log_sigmoid((kvf[..., 2 * D_MODEL:] + b_f).astype(jnp.float32)).astype(h.dtype)
    return k, v, logf


def fox_block(q_blk, cq_blk, qpos_blk, k, v, ck, kpos):
    s = jnp.einsum('bqhd,bkhd->bhqk', q_blk, k).astype(jnp.float32) * (B_HEAD_DIM ** -0.5)
    s = s + jnp.transpose(cq_blk, (0, 2, 1))[..., None] - jnp.transpose(ck, (0, 2, 1))[:, :, None, :]
    s = jnp.where(kpos[None, :] <= qpos_blk[:, None], s, NEG_INF)
    p = jax.nn.softmax(s, axis=-1).astype(v.dtype)
    return jnp.einsum('bhqk,bkhd->bqhd', p, v)


def fox_mixer(xn, k, v, logf, w_qg, g_q, w_o):
    bsz, t, _ = xn.shape
    n_k = k.shape[1]
    qg = xn @ w_qg
    q = rms_norm(qg[..., :D_MODEL].reshape(bsz, t, B_HEADS, B_HEAD_DIM), g_q)
    gate = qg[..., D_MODEL:]
    c = jnp.cumsum(logf.astype(jnp.float32), axis=1)
    cq = c[:, n_k - t:]
    kpos = jnp.arange(n_k)
    qpos = jnp.arange(n_k - t, n_k)
    blk = min(Q_BLOCK, t)
    n_blk = t // blk
    qb = q.reshape(bsz, n_blk, blk, B_HEADS, B_HEAD_DIM).transpose(1, 0, 2, 3, 4)
    cqb = cq.reshape(bsz, n_blk, blk, B_HEADS).transpose(1, 0, 2, 3)
    qposb = qpos.reshape(n_blk, blk)
    o = lax.map(lambda a: fox_block(a[0], a[1], a[2], k, v, c, kpos), (qb, cqb, qposb))
    o = o.transpose(1, 0, 2, 3, 4).reshape(bsz, t, D_MODEL)
    o = o * jax.nn.sigmoid(gate)
    return o @ w_o


def run_trunk(x, gla_s0, past_k, past_v, past_logf, chunk, ffn_norm, w_ffn_gu, w_ffn_down, mix_norm,
              a_w_in, a_w_g2, a_b_g, a_g_out, a_w_o, kv_norm, w_kvf, b_f, g_k, b_w_qg, b_g_q, b_w_o):
    h = x
    new_gla = []
    keys = None
    k_new = v_new = lf_new = None
    for layer in range(DEPTH):
        h = h + 0.5 * swiglu_ffn(rms_norm(h, ffn_norm[layer, 0]), w_ffn_gu[layer, 0], w_ffn_down[layer, 0])
        hn = rms_norm(h, mix_norm[layer])
        if layer < N_A_LAYERS:
            y, s_new = gla_mixer(hn, a_w_in[layer], a_w_g2[layer], a_b_g[layer], a_g_out[layer], a_w_o[layer],
                                 gla_s0[:, layer], chunk)
            new_gla.append(s_new)
        else:
            j = layer - N_A_LAYERS
            y = fox_mixer(hn, keys[0], keys[1], keys[2], b_w_qg[j], b_g_q[j], b_w_o[j])
        h = h + y
        h = h + 0.5 * swiglu_ffn(rms_norm(h, ffn_norm[layer, 1]), w_ffn_gu[layer, 1], w_ffn_down[layer, 1])
        if layer == N_A_LAYERS - 1:
            k_new, v_new, lf_new = shared_kv(h, kv_norm, w_kvf, b_f, g_k)
            if past_k is None:
                keys = (k_new, v_new, lf_new)
            else:
                keys = (jnp.concatenate([past_k, k_new], axis=1),
                        jnp.concatenate([past_v, v_new], axis=1),
                        jnp.concatenate([past_logf, lf_new], axis=1))
    return h, jnp.stack(new_gla, axis=1), k_new, v_new, lf_new


def setup_inputs(seed: int = 0) -> dict:
    key = jax.random.key(seed)
    ks = jax.random.split(key, 24)

    def nrm(k, shape, scale):
        return jax.random.normal(k, shape, jnp.float32) * scale

    def gain(k, shape):
        return 1.0 + 0.05 * jax.random.normal(k, shape, jnp.float32)

    return {
        'x_prompt': nrm(ks[0], (BATCH, SEQ, D_MODEL), 1.0),
        'x_sample': nrm(ks[1], (DEC_BATCH, DEC_SEQ, D_MODEL), 1.0),
        'state_gla': nrm(ks[2], (DEC_BATCH, N_A_LAYERS, A_HEADS, A_DK, A_DV), 1.0),
        'cache_k': nrm(ks[3], (DEC_BATCH, PAST_LEN, B_HEADS, B_HEAD_DIM), 1.0),
        'cache_v': nrm(ks[4], (DEC_BATCH, PAST_LEN, B_HEADS, B_HEAD_DIM), 1.0),
        'cache_logf': jax.nn.log_sigmoid(jax.random.uniform(ks[5], (DEC_BATCH, PAST_LEN, B_HEADS), jnp.float32, 1.0, 4.0)),
        'ffn_norm': gain(ks[6], (DEPTH, 2, D_MODEL)),
        'w_ffn_gu': nrm(ks[7], (DEPTH, 2, D_MODEL, 2 * D_FF), D_MODEL ** -0.5),
        'w_ffn_down': nrm(ks[8], (DEPTH, 2, D_FF, D_MODEL), D_FF ** -0.5),
        'mix_norm': gain(ks[9], (DEPTH, D_MODEL)),
        'a_w_in': nrm(ks[10], (N_A_LAYERS, D_MODEL, A_IN), D_MODEL ** -0.5),
        'a_w_g2': nrm(ks[11], (N_A_LAYERS, A_GATE_RANK, A_QK), A_GATE_RANK ** -0.5),
        'a_b_g': nrm(ks[12], (N_A_LAYERS, A_QK), 0.1),
        'a_g_out': gain(ks[13], (N_A_LAYERS, A_DV)),
        'a_w_o': nrm(ks[14], (N_A_LAYERS, A_VW, D_MODEL), A_VW ** -0.5),
        'kv_norm': gain(ks[15], (D_MODEL,)),
        'w_kvf': nrm(ks[16], (D_MODEL, 2 * D_MODEL + B_HEADS), D_MODEL ** -0.5),
        'b_f': jax.random.uniform(ks[17], (B_HEADS,), jnp.float32, 1.0, 4.0),
        'g_k': gain(ks[18], (B_HEAD_DIM,)),
        'b_w_qg': nrm(ks[19], (N_B_LAYERS, D_MODEL, 2 * D_MODEL), D_MODEL ** -0.5),
        'b_g_q': gain(ks[20], (N_B_LAYERS, B_HEAD_DIM)),
        'b_w_o': nrm(ks[21], (N_B_LAYERS, D_MODEL, D_MODEL), D_MODEL ** -0.5),
    }


def reference(x_prompt, x_sample, state_gla, cache_k, cache_v, cache_logf, ffn_norm, w_ffn_gu, w_ffn_down,
              mix_norm, a_w_in, a_w_g2, a_b_g, a_g_out, a_w_o, kv_norm, w_kvf, b_f, g_k, b_w_qg, b_g_q, b_w_o):
    s0_prompt = jnp.zeros((x_prompt.shape[0], N_A_LAYERS, A_HEADS, A_DK, A_DV), x_prompt.dtype)
    y_prompt, gla_prompt, k_prompt, v_prompt, lf_prompt = run_trunk(
        x_prompt, s0_prompt, None, None, None, CHUNK, ffn_norm, w_ffn_gu, w_ffn_down, mix_norm,
        a_w_in, a_w_g2, a_b_g, a_g_out, a_w_o, kv_norm, w_kvf, b_f, g_k, b_w_qg, b_g_q, b_w_o)
    y_sample, gla_sample, k_sample, v_sample, lf_sample = run_trunk(
        x_sample, state_gla, cache_k, cache_v, cache_logf, x_sample.shape[1], ffn_norm, w_ffn_gu, w_ffn_down,
        mix_norm, a_w_in, a_w_g2, a_b_g, a_g_out, a_w_o, kv_norm, w_kvf, b_f, g_k, b_w_qg, b_g_q, b_w_o)
    return (y_prompt, y_sample, gla_prompt, gla_sample, k_prompt, v_prompt, lf_prompt, k_sample, v_sample, lf_sample)
```

```python
import contextlib
import numpy as np
import concourse.bass as bass
import concourse.mybir as mybir
from concourse.bass_utils import run_bass_kernel_spmd

F32 = mybir.dt.float32
BF16 = mybir.dt.bfloat16
AF = mybir.ActivationFunctionType
ALU = mybir.AluOpType
AX = mybir.AxisListType

ENGS = ('pe', 'act', 'dve', 'pool', 'sp')

D = 1024
DFF = 2816
T = 4096
EPS = 1e-6
NEG = -30000.0


class Buf:
    __slots__ = ('name', 'last_w', 'readers', 'excl')

    def __init__(self, name, excl=False):
        self.name = name
        self.last_w = None
        self.readers = []
        self.excl = excl


class Op:
    __slots__ = ('eng', 'fn', 'deps', 'is_dma', 'dkey', 'dwaits', 'sig', 'ordv', 'idx', 'label')


class Prog:
    def __init__(self, nc):
        self.nc = nc
        self.ops = []
        self.dcnt = {}
        self.dfence = {}
        self.label = ''

    def buf(self, name, excl=False):
        return Buf(name, excl)

    def op(self, eng, fn, reads=(), writes=(), dma=None):
        o = Op()
        o.eng = eng
        o.fn = fn
        o.is_dma = dma is not None
        o.dkey = dma
        o.idx = len(self.ops)
        o.sig = False
        o.ordv = 0
        o.label = self.label
        deps = set()
        for b in reads:
            if b.last_w is not None:
                deps.add(b.last_w)
            if b.excl:
                for r in b.readers:
                    if self.ops[r].eng != eng:
                        deps.add(r)
        for b in writes:
            if b.last_w is not None:
                deps.add(b.last_w)
            deps.update(b.readers)
        best = {}
        for d in deps:
            p = self.ops[d]
            key = ('d', p.dkey) if p.is_dma else ('e', p.eng)
            if best.get(key, -1) < d:
                best[key] = d
        deps = set(best.values())
        o.deps = deps
        dw = {}
        for d in deps:
            p = self.ops[d]
            if p.is_dma:
                k = p.dkey
                v = self.dcnt[k]
                if dw.get(k, 0) < v:
                    dw[k] = v
                if self.dfence.get(k, 0) < v:
                    self.dfence[k] = v
        if o.is_dma:
            k = o.dkey
            f = self.dfence.get(k, 0)
            if f > 0 and dw.get(k, 0) < f:
                dw[k] = f
            self.dcnt[k] = self.dcnt.get(k, 0) + 1
        o.dwaits = dw
        for b in reads:
            b.readers.append(o.idx)
        for b in writes:
            b.last_w = o.idx
            b.readers = []
        self.ops.append(o)
        return o

    def frontier(self, bufs):
        best = {}
        for b in bufs:
            cand = list(b.readers)
            if b.last_w is not None:
                cand.append(b.last_w)
            for i in cand:
                p = self.ops[i]
                key = ('d', p.dkey) if p.is_dma else ('e', p.eng)
                if best.get(key, -1) < i:
                    best[key] = i
        return list(best.values())

    def emit(self, final_wait_eng='sp'):
        nc = self.nc
        ops = self.ops
        for o in ops:
            for d in o.deps:
                p = ops[d]
                if p.is_dma:
                    continue
                if p.eng == 'pe' and o.eng == 'pe' and not o.is_dma:
                    continue
                p.sig = True
        cnt = {e: 0 for e in ENGS}
        for o in ops:
            if not o.is_dma and o.sig:
                cnt[o.eng] += 1
                o.ordv = cnt[o.eng]
        with contextlib.ExitStack() as st:
            esem = {e: st.enter_context(nc.semaphore('s_' + e)) for e in ENGS if e != 'sp'}
            dsem = {k: st.enter_context(nc.semaphore('d_' + k)) for k in self.dcnt}
            block = st.enter_context(nc.Block())
            per = {e: [o for o in ops if o.eng == e] for e in ENGS}
            dtot = dict(self.dcnt)

            def run(eng_name, eng):
                known = {}
                for o in per[eng_name]:
                    need = {}
                    for d in o.deps:
                        p = ops[d]
                        if p.is_dma:
                            continue
                        if p.eng == 'pe' and eng_name == 'pe' and not o.is_dma:
                            continue
                        key = ('e', p.eng)
                        if need.get(key, 0) < p.ordv:
                            need[key] = p.ordv
                    for k, v in o.dwaits.items():
                        need[('d', k)] = v * 16
                    for key, v in need.items():
                        if known.get(key, 0) >= v:
                            continue
                        s = esem[key[1]] if key[0] == 'e' else dsem[key[1]]
                        eng.wait_ge(s, v)
                        known[key] = v
                    ins = o.fn(eng)
                    if o.is_dma:
                        ins.then_inc(dsem[o.dkey], 16)
                    elif o.sig:
                        ins.then_inc(esem[o.eng], 1)
                if eng_name == final_wait_eng:
                    for k, v in dtot.items():
                        eng.wait_ge(dsem[k], v * 16)
                    for e, c in cnt.items():
                        if e != 'sp' and c > 0:
                            eng.wait_ge(esem[e], c)

            block.tensor(lambda e: run('pe', e))
            block.scalar(lambda e: run('act', e))
            block.vector(lambda e: run('dve', e))
            block.gpsimd(lambda e: run('pool', e))
            block.sync(lambda e: run('sp', e))
        return {e: len(per[e]) for e in ENGS}, cnt


class RR:
    def __init__(self, items):
        self.items = list(items)
        self.i = 0

    def next(self):
        x = self.items[self.i % len(self.items)]
        self.i += 1
        return x


W8_INDEX = {}
_n = 0
for _l in range(2):
    for _i in range(2):
        W8_INDEX[('gu', _l, _i)] = _n
        _n += 11
W8_INDEX['win'] = _n; _n += 6
W8_INDEX['awo'] = _n; _n += 2
W8_INDEX['kvf'] = _n; _n += 4
W8_INDEX['qg'] = _n; _n += 4
W8_INDEX['bwo'] = _n; _n += 2
NP8 = _n
NPD = 32

SP_FFN = 0
SP_MIX = 32
SP_KVN = 48
SP_GOUT = 56
SP_GK = 58
SP_GQ = 122
SP_BF = 186
SP_GK2 = 202
SP_GQ2 = 203
NSP = 204

C_ID, C_TRIBLK, C_LOWBLK, C_TRIFULL, C_ONES, C_SEL127, C_SEL63, C_MASKNEG = range(8)


def _pieces8(w):
    n = w.shape[1] // 512
    return np.ascontiguousarray(w.reshape(8, 128, n, 512).transpose(2, 1, 0, 3))


def host_layout(inp):
    f = lambda a: np.asarray(a, dtype=np.float32)
    W8 = np.empty((NP8, 128, 8, 512), np.float32)
    WD = np.empty((NPD, 128, 22, 128), np.float32)
    wgu = f(inp['w_ffn_gu'])
    wdn = f(inp['w_ffn_down'])
    for l in range(2):
        for i in range(2):
            w = wgu[l, i].reshape(8, 128, 2, 11, 2, 128)
            W8[W8_INDEX[('gu', l, i)]:W8_INDEX[('gu', l, i)] + 11] = \
                w.transpose(3, 1, 0, 4, 2, 5).reshape(11, 128, 8, 512)
            d = wdn[l, i].reshape(22, 128, 8, 128)
            WD[(l * 2 + i) * 8:(l * 2 + i) * 8 + 8] = d.transpose(2, 1, 0, 3)
    win = f(inp['a_w_in'])[0]
    W8[W8_INDEX['win']:W8_INDEX['win'] + 6] = _pieces8(win[:, :3072])
    W8[W8_INDEX['awo']:W8_INDEX['awo'] + 2] = _pieces8(f(inp['a_w_o'])[0])
    wkvf = f(inp['w_kvf'])
    W8[W8_INDEX['kvf']:W8_INDEX['kvf'] + 4] = _pieces8(wkvf[:, :2048])
    W8[W8_INDEX['qg']:W8_INDEX['qg'] + 4] = _pieces8(f(inp['b_w_qg'])[0])
    W8[W8_INDEX['bwo']:W8_INDEX['bwo'] + 2] = _pieces8(f(inp['b_w_o'])[0])
    WS = np.empty((128, 2, 8, 16), np.float32)
    WS[:, 0] = win[:, 3072:3088].reshape(8, 128, 16).transpose(1, 0, 2)
    WS[:, 1] = wkvf[:, 2048:2064].reshape(8, 128, 16).transpose(1, 0, 2)
    WG2 = np.zeros((32, 512), np.float32)
    WG2[0:16] = f(inp['a_w_g2'])[0]
    WG2[16] = f(inp['a_b_g'])[0]
    SP = np.zeros((128, NSP), np.float32)
    fn = f(inp['ffn_norm'])
    for l in range(2):
        for i in range(2):
            SP[:, SP_FFN + (l * 2 + i) * 8:SP_FFN + (l * 2 + i) * 8 + 8] = fn[l, i].reshape(8, 128).T
    mn = f(inp['mix_norm'])
    for l in range(2):
        SP[:, SP_MIX + l * 8:SP_MIX + l * 8 + 8] = mn[l].reshape(8, 128).T
    SP[:, SP_KVN:SP_KVN + 8] = f(inp['kv_norm']).reshape(8, 128).T
    SP[:, SP_GOUT:SP_GOUT + 2] = f(inp['a_g_out'])[0].reshape(2, 128).T
    SP[:, SP_GK:SP_GK + 64] = f(inp['g_k'])[None, :]
    SP[:, SP_GQ:SP_GQ + 64] = f(inp['b_g_q'])[0][None, :]
    SP[:, SP_BF:SP_BF + 16] = f(inp['b_f'])[None, :]
    SP[:, SP_GK2] = np.tile(f(inp['g_k']), 2)
    SP[:, SP_GQ2] = np.tile(f(inp['b_g_q'])[0], 2)
    CN = np.zeros((128, 8, 128), np.float32)
    j = np.arange(128)[:, None]
    t = np.arange(128)[None, :]
    same = (j // 64) == (t // 64)
    CN[:, C_ID] = (j == t)
    CN[:, C_TRIBLK] = same & (j <= t)
    CN[:, C_LOWBLK] = same & (j > t)
    CN[:, C_TRIFULL] = (j <= t)
    CN[:, C_ONES] = 1.0
    CN[:, C_SEL127] = (j == 127) & (t >= 0)
    CN[:, C_SEL63] = (j == 63) & (t >= 0)
    CN[:, C_MASKNEG] = np.where(j <= t, 0.0, NEG)
    SEL = np.zeros((128, 16, 128), np.float32)
    for hh in range(16):
        e = hh % 2
        SEL[e * 64 + hh, hh, :] = 1.0
        SEL[e * 64 + 32 + hh, hh, :] = 1.0
    return dict(W8=W8.reshape(NP8, 128, 4096), WD=WD.reshape(NPD, 128, 2816), WS=WS.reshape(128, 256),
                WG2=WG2, SP=SP, CN=CN.reshape(128, 1024), SEL=SEL.reshape(128, 2048))


def build_program(NT=8, do_sample=True, dbg=False, STOP=99, KVF=255, KVS=9, SSTOP=99, CONV=1):
    nc = bass.Bass("TRN2", target_bir_lowering=False)
    P = Prog(nc)

    def din(name, shape, dt=F32):
        return nc.dram_tensor(name, shape, dt, kind="ExternalInput").ap()

    def dout(name, shape):
        return nc.dram_tensor(name, shape, F32, kind="ExternalOutput").ap()

    def dscr(name, shape, dt=BF16):
        return nc.dram_tensor(name, shape, dt, kind="Internal").ap()

    xp = din("xp", [T, D]); xs = din("xs", [128, D])
    sgla = din("sgla", [2, 4, 128, 256])
    ck = din("ck", [2, T, D]); cv = din("cv", [2, T, D]); clf = din("clf", [2, T, 16])
    W8f = din("W8", [NP8, 128, 4096]); WDf = din("WD", [NPD, 128, 2816])
    WSf = din("WS", [128, 256]); WG2f = din("WG2", [32, 512]); SPf = din("SP", [128, NSP]); CNf = din("CN", [128, 1024]); SELf = din("SEL", [128, 2048])
    yp = dout("yp", [T, D]); ys = dout("ys", [128, D])
    glap = dout("glap", [4, 128, 256]); glas = dout("glas", [2, 4, 128, 256])
    kp = dout("kp", [T, D]); vp = dout("vp", [T, D]); lfp = dout("lfp", [T, 16])
    kso = dout("kso", [128, D]); vso = dout("vso", [128, D]); lfs = dout("lfs", [128, 16])
    W8b = dscr("scr_w8", [NP8, 128, 4096]); WDb = dscr("scr_wd", [NPD, 128, 2816])
    KTp = dscr("scr_kt_p", [8, 128, T]); Vp = dscr("scr_v_p", [8, 128, 32, 132])
    KTs = [dscr(f"scr_kt_s{i}", [8, 128, T + 64]) for i in range(2)]
    Vs = [dscr(f"scr_v_s{i}", [8, 128, 33, 132]) for i in range(2)]
    bW8 = [P.buf(f"W8b{i}") for i in range(NP8)]
    bWD = [P.buf(f"WDb{i}") for i in range(NPD)]
    bKTp = P.buf("KTp"); bVp = P.buf("Vp")
    bKTs = [P.buf("KTs0"), P.buf("KTs1")]; bVs = [P.buf("Vs0"), P.buf("Vs1")]

    class TB:
        def __init__(self, name, shape, dt=F32):
            self.t = nc.alloc_sbuf_tensor(name, shape, dt)
            self.b = P.buf(name)
            self.name = name

        def __getitem__(self, k):
            return self.t[k]

    NSLOT = 4
    ws = [TB(f"ws{i}", [128, 4096], BF16) for i in range(NSLOT)]
    ws_rr = RR(range(NSLOT))
    h = TB("h", [128, 8, 512])
    xn = TB("xn", [128, 8, 512], BF16)
    xn.bs = [P.buf(f"xn{c_}") for c_ in range(8)]
    sq = TB("sq", [128, 8, 512], BF16)
    sq.bs = [P.buf(f"sq{c_}") for c_ in range(8)]
    lnv = TB("lnv", [128, 512]); rstd = TB("rstd", [128, 512])
    sel2 = TB("sel2", [128, 16, 128], BF16)
    Sst = [TB(f"S{i}", [128, 1024]) for i in range(3)]
    Sbf = [TB(f"Sbf{i}", [128, 1024], BF16) for i in range(3)]
    call_p = TB("call_p", [128, 32, 16])
    call_s = [TB(f"call_s{i}", [128, 33, 16]) for i in range(2)]
    carry_p = TB("carry_p", [128, 16])
    carry_s = [TB(f"carry_s{i}", [128, 16]) for i in range(2)]
    cn32 = TB("cn32", [128, 8, 128])
    idbf = TB("idbf", [128, 128], BF16); onesbf = TB("onesbf", [128, 128], BF16)
    mblkbf = TB("mblkbf", [128, 128], BF16); mnegbf = TB("mnegbf", [128, 128], BF16)
    spar = TB("spar", [128, NSP])
    wsm32 = TB("wsm32", [128, 256]); wsm = TB("wsm", [128, 2, 8, 16], BF16)
    wg2_32 = TB("wg2_32", [32, 512]); wg2 = TB("wg2", [32, 512], BF16)
    ARENA_B = 100 * 1024
    arena = nc.alloc_sbuf_tensor("arena", [128, ARENA_B // 2], BF16)
    ps = [nc.alloc_psum_tensor(f"ps{i}", [128, 512], F32) for i in range(8)]
    bps = [P.buf(f"ps{i}", excl=True) for i in range(8)]

    class AV:
        def __init__(self, ap, b):
            self.ap = ap
            self.b = b

        def __getitem__(self, k):
            return self.ap[k]

    arena_state = {'bufs': [], 'off': 0, 'front': []}

    def arena_phase():
        arena_state['front'] = P.frontier(arena_state['bufs'])
        arena_state['bufs'] = []
        arena_state['off'] = 0

    def aview(name, free_shape, dt, parts=128):
        esz = 4 if dt == F32 else 2
        n = int(np.prod(free_shape))
        nb = n * esz
        off = (arena_state['off'] + 31) // 32 * 32
        assert off + nb <= ARENA_B, (name, off, nb)
        arena_state['off'] = off + nb
        ap = arena[0:parts, off // 2:(off + nb) // 2]
        if dt == F32:
            ap = ap.bitcast(F32)
        if len(free_shape) == 2:
            ap = ap.rearrange("p (a b) -> p a b", b=free_shape[1])
        elif len(free_shape) == 3:
            ap = ap.rearrange("p (a b c) -> p a b c", b=free_shape[1], c=free_shape[2])
        b = P.buf(name)
        b.readers = list(arena_state['front'])
        arena_state['bufs'].append(b)
        return AV(ap, b)

    def mm(out, lhsT, rhs, start, stop, reads, writes):
        P.op('pe', lambda e: e.matmul(out, lhsT=lhsT, rhs=rhs, start=start, stop=stop, skip_group_check=True),
             reads=reads, writes=writes)

    def tr(out, in_, ident, reads, writes):
        P.op('pe', lambda e: e.transpose(out, in_, ident), reads=reads, writes=writes)

    def act(out, in_, func, reads, writes, bias=None, scale=1.0):
        if bias is None:
            P.op('act', lambda e: e.activation(out=out, in_=in_, func=func, scale=scale), reads=reads, writes=writes)
        else:
            P.op('act', lambda e: e.activation(out=out, in_=in_, func=func, bias=bias, scale=scale),
                 reads=reads, writes=writes)

    def cp(eng, out, in_, reads, writes):
        if eng == 'act':
            P.op('act', lambda e: e.copy(out=out, in_=in_), reads=reads, writes=writes)
        else:
            P.op(eng, lambda e: e.tensor_copy(out=out, in_=in_), reads=reads, writes=writes)

    def tt(eng, out, in0, in1, op, reads, writes):
        P.op(eng, lambda e: e.tensor_tensor(out=out, in0=in0, in1=in1, op=op), reads=reads, writes=writes)

    def stt(eng, out, in0, scalar, in1, op0, op1, reads, writes):
        P.op(eng, lambda e: e.scalar_tensor_tensor(out=out, in0=in0, scalar=scalar, in1=in1, op0=op0, op1=op1),
             reads=reads, writes=writes)

    def ts(eng, out, in0, s1, op0, reads, writes, s2=None, op1=None):
        if op1 is None:
            P.op(eng, lambda e: e.tensor_scalar(out=out, in0=in0, scalar1=s1, scalar2=None, op0=op0),
                 reads=reads, writes=writes)
        else:
            P.op(eng, lambda e: e.tensor_scalar(out=out, in0=in0, scalar1=s1, scalar2=s2, op0=op0, op1=op1),
                 reads=reads, writes=writes)

    def dma(q, out, in_, reads, writes, key):
        P.op(q, lambda e: e.dma_start(out=out, in_=in_), reads=reads, writes=writes, dma=key)

    def memset(eng, ap, val, writes):
        P.op(eng, lambda e: e.memset(ap, val), writes=writes)

    dma('sp', cn32[:].rearrange("p a b -> p (a b)"), CNf, [], [cn32.b], "cn32")
    dma('sp', spar[:], SPf, [], [spar.b], "spar")
    dma('sp', wsm32[:], WSf, [], [wsm32.b], "wsm32")
    dma('sp', wg2_32[:], WG2f, [], [wg2_32.b], "wg2_32")
    cp('dve', idbf[:], cn32[:, C_ID, :], [cn32.b], [idbf.b])
    cp('dve', onesbf[:], cn32[:, C_ONES, :], [cn32.b], [onesbf.b])
    cp('dve', mblkbf[:], cn32[:, C_TRIBLK, :], [cn32.b], [mblkbf.b])
    cp('dve', mnegbf[:], cn32[:, C_MASKNEG, :], [cn32.b], [mnegbf.b])
    cp('dve', wsm[:].rearrange("p a b c -> p (a b c)"), wsm32[:], [wsm32.b], [wsm.b])
    cp('dve', wg2[:], wg2_32[:], [wg2_32.b], [wg2.b])
    memset('dve', Sst[0][:], 0.0, [Sst[0].b])
    memset('dve', Sbf[0][:], 0.0, [Sbf[0].b])
    memset('dve', carry_p[:], 0.0, [carry_p.b])
    for i_ in range(2):
        memset('dve', call_s[i_][:, 32, :], 0.0, [call_s[i_].b])

    ident32 = cn32[:, C_ID, :]
    arena_phase()
    seltmp = aview("seltmp", [2048], F32)
    dma('sp', seltmp[:, :], SELf, [], [seltmp.b], "seltmp")
    cp('dve', sel2[:].rearrange("p a b -> p (a b)"), seltmp[:, :], [seltmp.b], [sel2.b])

    def cast8(i, key):
        dma('pool', W8b[i], W8f[i], [], [bW8[i]], key)

    def castd(i, key):
        dma('pool', WDb[i], WDf[i], [], [bWD[i]], key)

    def cast_ffn(l, i, first=False):
        base = W8_INDEX[('gu', l, i)]
        for j in range(11):
            key = f"cg{l}{i}"
            if first:
                key = "cgA0" if j < 1 else ("cgA1" if j < 3 else ("cgA2" if j < 6 else "cgA3"))
            cast8(base + j, key)
        for c in range(8):
            castd((l * 2 + i) * 8 + c, f"cd{l}{i}")

    def cast_group(name):
        n = {'win': 6, 'awo': 2, 'kvf': 4, 'qg': 4, 'bwo': 2}[name]
        for j in range(n):
            cast8(W8_INDEX[name] + j, "c" + name)

    wmode = {'first': False}

    def load8(idx):
        s = ws[ws_rr.next()]
        if wmode['first']:
            dma('pool', s[:, :], W8f[idx], [], [s.b], s.name)
            dma('sp', W8b[idx], s[:, :], [s.b], [bW8[idx]], s.name + "st")
        else:
            dma('sp', s[:, :], W8b[idx], [bW8[idx]], [s.b], s.name)
        return s, s[:, :].rearrange("p (k w) -> p k w", w=512)

    def loadd(idx):
        s = ws[ws_rr.next()]
        if wmode['first']:
            dma('pool', s[:, 0:2816], WDf[idx], [], [s.b], s.name)
            dma('sp', WDb[idx], s[:, 0:2816], [s.b], [bWD[idx]], s.name + "st")
        else:
            dma('sp', s[:, 0:2816], WDb[idx], [bWD[idx]], [s.b], s.name)
        return s, s[:, 0:2816].rearrange("p (m w) -> p m w", w=128)

    psG = RR([0, 1, 2, 3])
    evac_rr = RR(['act', 'dve'])

    stat_pend = []

    def stat_acc(c, N):
        act(sq[:, c, 0:N], h[:, c, 0:N], AF.Square, [h.b], [sq.bs[c]])
        for f_ in stat_pend:
            f_()
        del stat_pend[:]
        stat_pend.append(lambda: mm(ps[7][:, 0:N], onesbf[:], sq[:, c, 0:N], c == 0, c == 7,
                                    [onesbf.b, sq.bs[c]], [bps[7]]))
        if c == 7:
            for f_ in stat_pend:
                f_()
            del stat_pend[:]

    def norm(N, gcol, reuse=False):
        if not reuse:
            act(lnv[:, 0:N], ps[7][:, 0:N], AF.Ln, [bps[7]], [lnv.b], bias=EPS, scale=1.0 / D)
            act(rstd[:, 0:N], lnv[:, 0:N], AF.Exp, [lnv.b], [rstd.b], scale=-0.5)
        for c in range(8):
            stt('dve', xn[:, c, 0:N], h[:, c, 0:N], spar[:, gcol + c:gcol + c + 1], rstd[:, 0:N],
                ALU.mult, ALU.mult, [h.b, spar.b, rstd.b], [xn.bs[c]])

    def ffn(N, l, i):
        norm(N, SP_FFN + (l * 2 + i) * 8, reuse=(l == 1 and i == 0))
        arena_phase()
        hid = aview("hid", [22, 512], BF16)
        hid_bs = []
        for m_ in range(22):
            b_ = P.buf(f"hid{m_}")
            b_.readers = list(hid.b.readers)
            arena_state['bufs'].append(b_)
            hid_bs.append(b_)
        sg = [aview(f"sg{k}", [512], BF16) for k in range(2)]
        sg_rr = RR([0, 1])
        base = W8_INDEX[('gu', l, i)]
        for j in range(11):
            s, v = load8(base + j)
            banks = [(psG.next(), psG.next()) for mi in range(2)]
            if j == 0:
                for k in range(8):
                    for mi in range(2):
                        for gu in range(2):
                            mm(ps[banks[mi][gu]][:, 0:N], v[:, k, (mi * 2 + gu) * 128:(mi * 2 + gu + 1) * 128],
                               xn[:, k, 0:N], k == 0, k == 7, [s.b, xn.bs[k]], [bps[banks[mi][gu]]])
            for mi in range(2):
                m = 2 * j + mi
                bg, bu = banks[mi]
                if j != 0:
                    for k in range(8):
                        mm(ps[bg][:, 0:N], v[:, k, (mi * 2) * 128:(mi * 2 + 1) * 128], xn[:, k, 0:N], k == 0, k == 7,
                           [s.b, xn.bs[k]], [bps[bg]])
                    for k in range(8):
                        mm(ps[bu][:, 0:N], v[:, k, (mi * 2 + 1) * 128:(mi * 2 + 2) * 128], xn[:, k, 0:N], k == 0, k == 7,
                           [s.b, xn.bs[k]], [bps[bu]])
                g = sg[sg_rr.next()]
                act(g[:, 0:N], ps[bg][:, 0:N], AF.Silu, [bps[bg]], [g.b])
                tt('dve', hid[:, m, 0:N], g[:, 0:N], ps[bu][:, 0:N], ALU.mult, [g.b, bps[bu]], [hid_bs[m]])
        for c in range(8):
            s, v = loadd((l * 2 + i) * 8 + c)
            b = psG.next()
            for m in range(22):
                mm(ps[b][:, 0:N], v[:, m, :], hid[:, m, 0:N], m == 0, m == 21, [s.b, hid_bs[m]], [bps[b]])
            stt('dve', h[:, c, 0:N], ps[b][:, 0:N], 0.5, h[:, c, 0:N], ALU.mult, ALU.add, [bps[b], h.b], [h.b])
            stat_acc(c, N)

    def gla(N, chunk_states):
        nsub = N // 128
        nch = N // 64
        norm(N, SP_MIX + 0)
        arena_phase()
        qk = aview("qk", [8, 512], BF16)
        silur = aview("silur", [8, 512], BF16)
        ktm = aview("ktm", [4, 512], BF16)
        vtm = aview("vtm", [4, 1024], BF16)
        sptm = aview("sptm", [4, 512], F32)
        etmp = aview("etmp", [512], F32)
        e1 = [aview(f"e1_{k}", [512], F32) for k in range(2)]
        e2 = [aview(f"e2_{k}", [512], F32) for k in range(2)]
        fb = [aview(f"fb{k}", [512], F32) for k in range(2)]
        am = aview("am", [4, 512], BF16)
        osb = aview("osb", [8, 512], F32)
        glaug = aview("glaug", [512], BF16, parts=32)
        dec = aview("dec", [4, 8], F32)
        rsh = [aview(f"rsh{k}", [512], F32) for k in range(2)]
        wi = W8_INDEX['win']
        for pi in range(2):
            s, v = load8(wi + pi)
            bb4 = [psG.next() for hh in range(4)]
            if pi == 0:
                for k in range(8):
                    for hh in range(4):
                        mm(ps[bb4[hh]][:, 0:N], v[:, k, hh * 128:(hh + 1) * 128], xn[:, k, 0:N], k == 0, k == 7,
                           [s.b, xn.bs[k]], [bps[bb4[hh]]])
            for hh in range(4):
                b = bb4[hh]
                if pi != 0:
                    for k in range(8):
                        mm(ps[b][:, 0:N], v[:, k, hh * 128:(hh + 1) * 128], xn[:, k, 0:N], k == 0, k == 7,
                           [s.b, xn.bs[k]], [bps[b]])
                cp(evac_rr.next(), qk[:, pi * 4 + hh, 0:N], ps[b][:, 0:N], [bps[b]], [qk.b])
            if pi == 1:
                for sub in range(nsub):
                    b = psG.next()
                    for k in range(8):
                        mm(ps[b][:, :], xn[:, k, sub * 128:(sub + 1) * 128], v[:, k, :], k == 0, k == 7,
                           [s.b, xn.bs[k]], [bps[b]])
                    cp(evac_rr.next(), ktm[:, sub, :], ps[b][:, :], [bps[b]], [ktm.b])
        for pi in range(2):
            s, v = load8(wi + 2 + pi)
            for sub in range(nsub):
                b = psG.next()
                for k in range(8):
                    mm(ps[b][:, :], xn[:, k, sub * 128:(sub + 1) * 128], v[:, k, :], k == 0, k == 7,
                       [s.b, xn.bs[k]], [bps[b]])
                cp(evac_rr.next(), vtm[:, sub, pi * 512:(pi + 1) * 512], ps[b][:, :], [bps[b]], [vtm.b])
        r_state = {}

        def emit_r_groups(ch):
            gpc = 8 // nch
            for gi in range(ch * gpc, (ch + 1) * gpc):
                pi, hh = gi // 4, gi % 4
                if pi not in r_state:
                    r_state[pi] = load8(wi + 4 + pi)
                s, v = r_state[pi]
                b = psG.next()
                for k in range(8):
                    mm(ps[b][:, 0:N], v[:, k, hh * 128:(hh + 1) * 128], xn[:, k, 0:N], k == 0, k == 7,
                       [s.b, xn.bs[k]], [bps[b]])
                act(silur[:, gi, 0:N], ps[b][:, 0:N], AF.Silu, [bps[b]], [silur.b])
        memset('dve', glaug[0:32, 0:N], 1.0, [glaug.b])
        b = psG.next()
        for k in range(8):
            mm(ps[b][0:16, 0:N], wsm[:, 0, k, :], xn[:, k, 0:N], k == 0, k == 7, [wsm.b, xn.bs[k]], [bps[b]])
        cp('dve', glaug[0:16, 0:N], ps[b][0:16, 0:N], [bps[b]], [glaug.b])
        for sub in range(nsub):
            b = psG.next()
            mm(ps[b][:, :], glaug[0:32, sub * 128:(sub + 1) * 128], wg2[0:32, :], True, True, [glaug.b, wg2.b], [bps[b]])
            act(etmp[:, :], ps[b][:, :], AF.Exp, [bps[b]], [etmp.b], scale=-1.0)
            act(sptm[:, sub, :], etmp[:, :], AF.Ln, [etmp.b], [sptm.b], bias=1.0)
        for hh in range(4):
            b = psG.next()
            for sub in range(nsub):
                mm(ps[b][:, sub * 128:(sub + 1) * 128], sptm[:, sub, hh * 128:(hh + 1) * 128], cn32[:, C_TRIBLK, :],
                   True, True, [sptm.b, cn32.b], [bps[b]])
            a1 = e1[hh % 2]
            a2 = e2[hh % 2]
            act(a1[:, 0:N], ps[b][:, 0:N], AF.Exp, [bps[b]], [a1.b], scale=-1.0 / 16.0)
            act(a2[:, 0:N], ps[b][:, 0:N], AF.Exp, [bps[b]], [a2.b], scale=1.0 / 16.0)
            stt('dve', qk[:, hh, 0:N], qk[:, hh, 0:N], 128.0 ** -0.5, a1[:, 0:N], ALU.mult, ALU.mult,
                [qk.b, a1.b], [qk.b])
            tt('pool', qk[:, 4 + hh, 0:N], qk[:, 4 + hh, 0:N], a2[:, 0:N], ALU.mult, [qk.b, a2.b], [qk.b])
            cp('dve', dec[:, hh, 0:nch], a1[:, 0:N].rearrange("p (c t) -> p c t", t=64)[:, :, 63], [a1.b], [dec.b])
        for sub in range(nsub):
            b = psG.next()
            mm(ps[b][:, :], cn32[:, C_LOWBLK, :], sptm[:, sub, :], True, True, [cn32.b, sptm.b], [bps[b]])
            f_ = fb[sub % 2]
            act(f_[:, :], ps[b][:, :], AF.Exp, [bps[b]], [f_.b], scale=-1.0 / 16.0)
            tt('dve', ktm[:, sub, :], ktm[:, sub, :], f_[:, :], ALU.mult, [ktm.b, f_.b], [ktm.b])
        for hh in range(4):
            b = psG.next()
            for sub in range(nsub):
                mm(ps[b][:, sub * 128:(sub + 1) * 128], qk[:, 4 + hh, sub * 128:(sub + 1) * 128],
                   qk[:, hh, sub * 128:(sub + 1) * 128], True, True, [qk.b], [bps[b]])
            tt('dve', am[:, hh, 0:N].rearrange("p (s t) -> p s t", t=128),
               ps[b][:, 0:N].rearrange("p (s t) -> p s t", t=128),
               mblkbf[:].unsqueeze(1).broadcast_to([128, nsub, 128]), ALU.mult, [bps[b], mblkbf.b], [am.b])
        psO = RR([4, 5])
        for ch in range(nch):
            sub = ch // 2
            r0 = (ch % 2) * 64
            S, Sb = chunk_states[ch]
            bo = psO.next()
            for hh in range(4):
                for dvc in range(2):
                    col = (hh * 2 + dvc) * 64
                    mm(ps[bo][:, col:col + 64], Sb[:, hh * 256 + dvc * 128:hh * 256 + (dvc + 1) * 128],
                       qk[:, hh, ch * 64:(ch + 1) * 64], True, False, [Sb.b, qk.b], [bps[bo]])
                    mm(ps[bo][:, col:col + 64], vtm[:, sub, hh * 256 + dvc * 128:hh * 256 + (dvc + 1) * 128],
                       am[:, hh, sub * 128 + r0:sub * 128 + r0 + 64], False, True, [vtm.b, am.b], [bps[bo]])
            cp('act', osb[:, :, ch * 64:(ch + 1) * 64], ps[bo][:, :].rearrange("p (a t) -> p a t", t=64),
               [bps[bo]], [osb.b])
            for hh in range(4):
                bd = 6 + hh // 2
                c0 = (hh % 2) * 256
                mm(ps[bd][:, c0:c0 + 256], ktm[r0:r0 + 64, sub, hh * 128:(hh + 1) * 128],
                   vtm[r0:r0 + 64, sub, hh * 256:(hh + 1) * 256], True, True, [ktm.b, vtm.b], [bps[bd]])
            for hh in range(4):
                bd = 6 + hh // 2
                c0 = (hh % 2) * 256
                stt('dve', S[:, hh * 256:(hh + 1) * 256], S[:, hh * 256:(hh + 1) * 256], dec[:, hh, ch:ch + 1],
                    ps[bd][:, c0:c0 + 256], ALU.mult, ALU.add, [S.b, dec.b, bps[bd]], [S.b])
            cp('act', Sb[:, :], S[:, :], [S.b], [Sb.b])
            emit_r_groups(ch)
        tq_bs = []
        qk_front = P.frontier([qk.b])
        for c in range(8):
            b_ = P.buf(f"tq{c}")
            b_.readers = list(qk_front)
            arena_state['bufs'].append(b_)
            tq_bs.append(b_)
        for c in range(8):
            tt('pool' if c % 2 == 0 else 'dve', qk[:, c, 0:N], osb[:, c, 0:N], silur[:, c, 0:N], ALU.mult,
               [osb.b, silur.b], [tq_bs[c]])
        for hh in range(4):
            act(sq[:, 2 * hh:2 * hh + 2, 0:N], osb[:, 2 * hh:2 * hh + 2, 0:N], AF.Square, [osb.b],
                [sq.bs[2 * hh], sq.bs[2 * hh + 1]])
            b = psG.next()
            for dvc in range(2):
                mm(ps[b][:, 0:N], onesbf[:], sq[:, hh * 2 + dvc, 0:N], dvc == 0, dvc == 1,
                   [onesbf.b, sq.bs[hh * 2 + dvc]], [bps[b]])
            act(lnv[:, 0:N], ps[b][:, 0:N], AF.Ln, [bps[b]], [lnv.b], bias=EPS, scale=1.0 / 256.0)
            r_ = rsh[hh % 2]
            act(r_[:, 0:N], lnv[:, 0:N], AF.Exp, [lnv.b], [r_.b], scale=-0.5)
            for dvc in range(2):
                c = hh * 2 + dvc
                stt('dve', xn[:, c, 0:N], qk[:, c, 0:N], spar[:, SP_GOUT + dvc:SP_GOUT + dvc + 1], r_[:, 0:N],
                    ALU.mult, ALU.mult, [tq_bs[c], spar.b, r_.b], [xn.bs[c]])
        for pi in range(2):
            s, v = load8(W8_INDEX['awo'] + pi)
            bb4 = [psG.next() for cc in range(4)]
            if pi == 0:
                for k in range(8):
                    for cc in range(4):
                        mm(ps[bb4[cc]][:, 0:N], v[:, k, cc * 128:(cc + 1) * 128], xn[:, k, 0:N], k == 0, k == 7,
                           [s.b, xn.bs[k]], [bps[bb4[cc]]])
            for cc in range(4):
                c = pi * 4 + cc
                b = bb4[cc]
                if pi != 0:
                    for k in range(8):
                        mm(ps[b][:, 0:N], v[:, k, cc * 128:(cc + 1) * 128], xn[:, k, 0:N], k == 0, k == 7,
                           [s.b, xn.bs[k]], [bps[b]])
                tt('dve', h[:, c, 0:N], ps[b][:, 0:N], h[:, c, 0:N], ALU.add, [bps[b], h.b], [h.b])
                stat_acc(c, N)

    def headnorm(b0, b1, gcol, raw, sqt, ssk, lnk, rk, out=None, rows=128):
        R = slice(0, rows)
        act(sqt[R, 0:512], ps[b0][R, :], AF.Square, [bps[b0]], [sqt.b])
        act(sqt[R, 512:1024], ps[b1][R, :], AF.Square, [bps[b1]], [sqt.b])
        P.op('dve', lambda e: e.tensor_reduce(out=ssk[R, :], in_=sqt[R, :].rearrange("p (a b) -> p a b", b=64),
                                              axis=AX.X, op=ALU.add), reads=[sqt.b], writes=[ssk.b])
        act(lnk[R, :], ssk[R, :], AF.Ln, [ssk.b], [lnk.b], bias=EPS, scale=1.0 / 64.0)
        act(rk[R, :], lnk[R, :], AF.Exp, [lnk.b], [rk.b], scale=-0.5)
        for hf, bb in enumerate((b0, b1)):
            tt('dve', raw[R, hf * 512:(hf + 1) * 512].rearrange("p (a b) -> p a b", b=64),
               ps[bb][R, :].rearrange("p (a b) -> p a b", b=64),
               rk[R, hf * 8:(hf + 1) * 8].unsqueeze(2).broadcast_to([rows, 8, 64]), ALU.mult,
               [bps[bb], rk.b], [raw.b])
        if out is not None:
            tt('pool', out[R, :].rearrange("p (a b) -> p a b", b=64), raw[R, :].rearrange("p (a b) -> p a b", b=64),
               spar[R, gcol:gcol + 64].unsqueeze(1).broadcast_to([rows, 16, 64]), ALU.mult, [raw.b, spar.b], [out.b])

    psT = RR([5, 6])

    def transposes_to(src, dst_fn, reads, dst_b, rows=128, gcol=None):
        for half in range(2):
            b = psT.next()
            for i in range(4):
                pr = half * 4 + i
                tr(ps[b][:, i * 128:i * 128 + rows], src[0:rows, pr * 128:(pr + 1) * 128], ident32[0:rows, 0:rows],
                   reads + [cn32.b], [bps[b]])
            src_v = ps[b][:, :].rearrange("p (a t) -> p a t", t=128)[:, :, 0:rows]
            if gcol is None:
                cp(evac_rr.next(), dst_fn(half * 4), src_v, [bps[b]], [dst_b])
            elif half == 0:
                act(dst_fn(half * 4), src_v, AF.Copy, [bps[b], spar.b], [dst_b], scale=spar[:, gcol:gcol + 1])
            else:
                ts('dve', dst_fn(half * 4), src_v, spar[:, gcol:gcol + 1], ALU.mult, [bps[b], spar.b], [dst_b])

    def cumsum_tile(L, nsub, carry, call, ks0, psb, pref, tri=C_TRIFULL):
        n16 = nsub * 16
        Lf = L[:, 0:nsub, :].rearrange("p a b -> p (a b)")
        mm(ps[psb][:, 0:n16], cn32[:, tri, :], Lf, True, True, [cn32.b, L.b], [bps[psb]])
        mm(ps[psb][:, 256:256 + n16], cn32[:, C_ONES, :], Lf, True, True, [cn32.b, L.b], [bps[psb]])
        cp('dve', pref[:, 0, :], carry[:, :], [carry.b], [pref.b])
        for i in range(1, nsub):
            tt('dve', pref[:, i, :], pref[:, i - 1, :], ps[psb][:, 256 + (i - 1) * 16:256 + i * 16], ALU.add,
               [pref.b, bps[psb]], [pref.b])
        tt('dve', carry[:, :], pref[:, nsub - 1, :], ps[psb][:, 256 + (nsub - 1) * 16:256 + nsub * 16], ALU.add,
           [pref.b, bps[psb]], [carry.b])
        tt('dve', call[:, ks0:ks0 + nsub, :], ps[psb][:, 0:n16].rearrange("p (a b) -> p a b", b=16),
           pref[:, 0:nsub, :], ALU.add, [bps[psb], pref.b], [call.b])

    def kv(N, tile, sample):
        nsub = N // 128
        norm(N, SP_KVN)
        arena_phase()
        kout = [aview(f"kout{k}", [1024], F32) for k in range(2)]
        vout = [aview(f"vout{k}", [1024], F32) for k in range(2)]
        sqt = aview("sqt", [1024], F32)
        kraw = aview("kraw", [1024], F32)
        vbt = aview("vbt", [8, nsub, 132], BF16)
        vb5 = vbt.ap.rearrange("p pr s (e c) -> p pr s e c", e=2)
        ktn = aview("ktn", [8, 512], BF16)
        Lt = aview("Lt", [4, 16], F32)
        ssk = aview("ssk", [16], F32); lnk = aview("lnk", [16], F32); rk = aview("rk", [16], F32)
        zz = aview("zz", [16], F32); ez = aview("ez", [16], F32)
        pref = aview("pref", [4, 16], F32)
        memset('pool', vbt[:, :, :, :], 1.0, [vbt.b])
        wk = W8_INDEX['kvf']
        pieces = [load8(wk + j) for j in range(4)]
        for sub in range(nsub):
            tok0 = tile * 512 + sub * 128
            ko = kout[sub % 2]; vo = vout[sub % 2]
            kb = []
            for hf in range(2):
                b = psG.next()
                s, v = pieces[hf]
                for k in range(8):
                    mm(ps[b][:, :], xn[:, k, sub * 128:(sub + 1) * 128], v[:, k, :], k == 0, k == 7,
                       [s.b, xn.bs[k]], [bps[b]])
                kb.append(b)
            for k in range(8):
                mm(ps[4][:, 0:16], xn[:, k, sub * 128:(sub + 1) * 128], wsm[:, 1, k, :], k == 0, k == 7,
                   [wsm.b, xn.bs[k]], [bps[4]])
            if KVS < 2:
                continue
            headnorm(kb[0], kb[1], SP_GK, kraw, sqt, ssk, lnk, rk, out=ko)
            if KVS < 3:
                continue
            if not sample:
                if KVF & 1:
                    dma('pool', kp[tok0:tok0 + 128, :], ko[:, :], [ko.b], [], ko.b.name)
            else:
                dma('pool', kso[:, :], ko[:, :], [ko.b], [], ko.b.name)
            vbk = []
            for hf in range(2):
                b = psG.next()
                s, v = pieces[2 + hf]
                for k in range(8):
                    mm(ps[b][:, :], xn[:, k, sub * 128:(sub + 1) * 128], v[:, k, :], k == 0, k == 7,
                       [s.b, xn.bs[k]], [bps[b]])
                vbk.append(b)
            transposes_to(kraw, lambda pr0: ktn[:, pr0:pr0 + 4, sub * 128:(sub + 1) * 128], [kraw.b], ktn.b,
                          gcol=SP_GK2)
            for hf in range(2):
                if KVF & 64:
                    cp('act', vo[:, hf * 512:(hf + 1) * 512], ps[vbk[hf]][:, :], [bps[vbk[hf]]], [vo.b])
                cp('dve', vb5[:, hf * 4:(hf + 1) * 4, sub, :, 0:64],
                   ps[vbk[hf]][:, :].rearrange("p (pr e d) -> p pr e d", pr=4, e=2), [bps[vbk[hf]]], [vbt.b])
            if not sample:
                if KVF & 2:
                    dma('pool', vp[tok0:tok0 + 128, :], vo[:, :], [vo.b], [], vo.b.name)
            else:
                dma('pool', vso[:, :], vo[:, :], [vo.b], [], vo.b.name)
            if KVS < 5:
                continue
            tt('dve', zz[:, :], ps[4][:, 0:16], spar[:, SP_BF:SP_BF + 16], ALU.add, [bps[4], spar.b], [zz.b])
            act(ez[:, :], zz[:, :], AF.Exp, [zz.b], [ez.b], scale=-1.0)
            act(zz[:, :], ez[:, :], AF.Ln, [ez.b], [zz.b], bias=1.0)
            ts('dve', Lt[:, sub, :], zz[:, :], -1.0, ALU.mult, [zz.b], [Lt.b])
        if not sample:
            dma('pool', Vp[:, :, tile * 4:tile * 4 + 4, :].rearrange("pr p s c -> p pr s c"), vbt[:, :, :, :],
                [vbt.b], [bVp], vbt.b.name)
        else:
            for sq_ in range(2):
                dma('pool', Vs[sq_][:, 0:64, 32:33, :].rearrange("pr p s c -> p pr s c"),
                    vbt[sq_ * 64:(sq_ + 1) * 64, :, :, :], [vbt.b], [bVs[sq_]], vbt.b.name)
        if not sample:
            if KVF & 8:
                dma('pool', lfp[tile * 512:(tile + 1) * 512, :].rearrange("(s p) c -> p s c", p=128), Lt[:, 0:nsub, :],
                    [Lt.b], [], Lt.b.name)
            if KVF & 16:
                dma('pool', KTp[:, :, tile * 512:(tile + 1) * 512].rearrange("pr p t -> p pr t"), ktn[:, :, :],
                    [ktn.b], [bKTp], ktn.b.name)
            if KVF & 32:
                cumsum_tile(Lt, 4, carry_p, call_p, tile * 4, 4, pref)
        else:
            dma('pool', lfs[:, :], Lt[:, 0, :], [Lt.b], [], Lt.b.name)
            for sq_ in range(2):
                dma('pool', KTs[sq_][:, :, T:T + 64].rearrange("pr p t -> p pr t"),
                    ktn[:, :, sq_ * 64:(sq_ + 1) * 64], [ktn.b], [bKTs[sq_]], ktn.b.name)
            mm(ps[4][:, 32:48], cn32[:, C_TRIBLK, :], Lt[:, 0, :], True, True, [cn32.b, Lt.b], [bps[4]])
            cp('dve', pref[0:64, 0, :], carry_s[0][0:64, :], [carry_s[0].b], [pref.b])
            cp('dve', pref[64:128, 0, :], carry_s[1][64:128, :], [carry_s[1].b], [pref.b])
            tt('dve', pref[:, 1, :], ps[4][:, 32:48], pref[:, 0, :], ALU.add, [bps[4], pref.b], [pref.b])
            cp('dve', call_s[0][0:64, 32, :], pref[0:64, 1, :], [pref.b], [call_s[0].b])
            cp('dve', call_s[1][0:64, 32, :], pref[64:128, 1, :], [pref.b], [call_s[1].b])

    def fox(N, tile, sample):
        nsub = N // 128
        norm(N, SP_MIX + 8)
        arena_phase()
        qn = [aview("qn0", [1024], F32)]
        sqt = aview("sqt", [1024], F32)
        ssk = aview("ssk", [16], F32); lnk = aview("lnk", [16], F32); rk = aview("rk", [16], F32)
        QT = aview("QT", [16, 512], BF16)
        QT4 = QT.ap.rearrange("p (pr e) t -> p pr e t", e=2)
        memset('pool', QT[:, :, :], 0.0, [QT.b])
        sgate = aview("sgate", [8, 512], BF16)
        Aall = aview("Aall", [8, 512], BF16)
        Aall_bs = []
        for m_ in range(8):
            b_ = P.buf(f"Aall{m_}")
            b_.readers = list(Aall.b.readers)
            arena_state['bufs'].append(b_)
            Aall_bs.append(b_)
        NKMAX = T + 64 if sample else (tile + 1) * 512
        NKS = 33 if sample else (tile + 1) * 4
        ktp = [aview(f"ktp{k}", [NKMAX], BF16) for k in range(2)]
        vpr = [aview(f"vpr{k}", [NKS, 132], BF16) for k in range(2)]
        pt = [aview(f"pt{k}", [512], BF16) for k in range(6)]
        pt_rr = RR(range(6))
        LOOK = 3 if sample else 2
        Rb = aview("Rb", [4, 16], F32)
        HL = aview("HL", [512], BF16)
        cqv = aview("cqv", [512], F32, parts=16)
        rcol = aview("rcol", [8], F32, parts=16)
        memset('pool', HL[:, :], 0.0, [HL.b])
        rden = [aview(f"rden{k}", [512], F32) for k in range(2)]
        tO = [aview(f"tO{k}", [512], F32) for k in range(2)]
        wq = W8_INDEX['qg']
        pieces = [load8(wq + j) for j in range(2)]
        gate_state = {'piece': {}}

        def emit_gate_groups(sub):
            gps = 8 // nsub
            for gi in range(sub * gps, (sub + 1) * gps):
                pi, cc = gi // 4, gi % 4
                if pi not in gate_state['piece']:
                    gate_state['piece'][pi] = load8(wq + 2 + pi)
                s, v = gate_state['piece'][pi]
                b = psG.next()
                for k in range(8):
                    mm(ps[b][:, 0:N], v[:, k, cc * 128:(cc + 1) * 128], xn[:, k, 0:N], k == 0, k == 7,
                       [s.b, xn.bs[k]], [bps[b]])
                act(sgate[:, gi, 0:N], ps[b][:, 0:N], AF.Sigmoid, [bps[b]], [sgate.b])

        for sub in range(nsub):
            q_ = qn[0]
            qb = []
            for hf in range(2):
                b = psG.next()
                s, v = pieces[hf]
                for k in range(8):
                    mm(ps[b][:, :], xn[:, k, sub * 128:(sub + 1) * 128], v[:, k, :], k == 0, k == 7,
                       [s.b, xn.bs[k]], [bps[b]])
                qb.append(b)
            headnorm(qb[0], qb[1], SP_GQ, q_, sqt, ssk, lnk, rk)
            emit_gate_groups(sub)
            for half in range(2):
                b = psT.next()
                for i in range(4):
                    pr = half * 4 + i
                    tr(ps[b][:, i * 128:(i + 1) * 128], q_[:, pr * 128:(pr + 1) * 128], ident32, [q_.b, cn32.b], [bps[b]])
                pv = ps[b][:, :].rearrange("p (a t) -> p a t", t=128)
                act(QT4[0:64, half * 4:half * 4 + 4, 0, sub * 128:(sub + 1) * 128], pv[0:64], AF.Copy,
                    [bps[b], spar.b], [QT.b], scale=spar[0:64, SP_GQ2:SP_GQ2 + 1])
                ts('dve', QT4[64:128, half * 4:half * 4 + 4, 1, sub * 128:(sub + 1) * 128], pv[64:128],
                   spar[64:128, SP_GQ2:SP_GQ2 + 1], ALU.mult, [bps[b], spar.b], [QT.b])
        psS = RR([0, 1, 2, 6, 7]) if sample else RR([0, 1, 2])
        psO = RR([3, 4])
        if not sample:
            seqs = [dict(q0=0, nq=512, KT=KTp, bKT=bKTp, V=Vp, bV=bVp, call=call_p, nfull=tile * 4, ndiag=4,
                         sel=C_SEL127, crow0=0)]
        else:
            seqs = [dict(q0=sq_ * 64, nq=64, KT=KTs[sq_], bKT=bKTs[sq_], V=Vs[sq_], bV=bVs[sq_], call=call_s[sq_],
                         nfull=32, ndiag=1, sel=C_SEL63, crow0=0) for sq_ in range(2)]
        for sd in seqs:
            q0, nq, call = sd['q0'], sd['nq'], sd['call']
            nfull, ndiag = sd['nfull'], sd['ndiag']
            nks = nfull + ndiag
            qsub = min(128, nq)
            nj = nq // qsub
            kw = qsub
            nk = nfull * 128 + ndiag * kw
            mm(ps[5][:, 0:nj * 16], cn32[:, sd['sel'], :],
               call[:, nfull:nfull + nj, :].rearrange("p a b -> p (a b)"), True, True, [cn32.b, call.b], [bps[5]])
            cp('dve', Rb[:, 0:nj, :].rearrange("p a b -> p (a b)"), ps[5][:, 0:nj * 16], [bps[5]], [Rb.b])
            bias = aview(f"bias{q0}", [nks, 16], F32)
            tt('dve', bias[:, 0:nks, :], Rb[:, nj - 1, :].unsqueeze(1).broadcast_to([128, nks, 16]),
               call[:, 0:nks, :], ALU.subtract, [Rb.b, call.b], [bias.b])
            for j in range(nj):
                tr(ps[6][0:16, j * qsub:(j + 1) * qsub], call[0:qsub, nfull + j, :], ident32[0:qsub, 0:qsub],
                   [call.b, cn32.b], [bps[6]])
            cp('dve', rcol[0:16, 0:1], ps[6][0:16, nq - 1:nq], [bps[6]], [rcol.b])
            ts('dve', cqv[0:16, 0:nq], ps[6][0:16, 0:nq], rcol[0:16, 0:1], ALU.subtract, [bps[6], rcol.b], [cqv.b],
               s2=8.0, op1=ALU.mult)
            cp('dve', HL[0:16, q0:q0 + nq], cqv[0:16, 0:nq], [cqv.b], [HL.b])
            cp('dve', HL[64:80, q0:q0 + nq], cqv[0:16, 0:nq], [cqv.b], [HL.b])
            tt('dve', HL[32:48, q0:q0 + nq], cqv[0:16, 0:nq], HL[0:16, q0:q0 + nq], ALU.subtract, [cqv.b, HL.b], [HL.b])
            tt('dve', HL[96:112, q0:q0 + nq], cqv[0:16, 0:nq], HL[0:16, q0:q0 + nq], ALU.subtract, [cqv.b, HL.b], [HL.b])
            pending = []
            for pr in range(8):
                kt_ = ktp[pr % 2]; vp_ = vpr[pr % 2]
                dma('sp', kt_[:, 0:nk], sd['KT'][pr, :, 0:nk], [sd['bKT']], [kt_.b], kt_.b.name)
                dma('sp', vp_[:, 0:nks, :], sd['V'][pr, :, 0:nks, :], [sd['bV']], [vp_.b], vp_.b.name)
                for e_ in range(2):
                    hd = pr * 2 + e_
                    rows = slice(e_ * 64, (e_ + 1) * 64)
                    bo = psO.next()
                    for ks in range(nks):
                        diag = ks >= nfull
                        jd = ks - nfull
                        qs = jd * qsub if diag else 0
                        kn = kw if diag else 128
                        bs = psS.next()
                        mm(ps[bs][0:kn, qs:nq], kt_[:, ks * 128:ks * 128 + kn], QT[:, hd, q0 + qs:q0 + nq],
                           True, False, [kt_.b, QT.b], [bps[bs]])
                        mm(ps[bs][0:kn, qs:nq], sel2[:, hd, 0:kn], HL[:, q0 + qs:q0 + nq],
                           False, not diag, [sel2.b, HL.b], [bps[bs]])
                        if diag:
                            if kn == 128:
                                mm(ps[bs][0:kn, qs:qs + qsub], idbf[:, :], mnegbf[:, 0:qsub], False, True,
                                   [idbf.b, mnegbf.b], [bps[bs]])
                            else:
                                mm(ps[bs][0:kn, qs:qs + qsub], idbf[0:kn, 0:kn], mnegbf[0:kn, 0:qsub], False, True,
                                   [idbf.b, mnegbf.b], [bps[bs]])
                        p_ = pt[pt_rr.next()]
                        act(p_[0:kn, qs:nq], ps[bs][0:kn, qs:nq], AF.Exp, [bps[bs], bias.b], [p_.b],
                            bias=bias[0:kn, ks, hd:hd + 1], scale=0.125)
                        nxt = []
                        for cd_, f_ in pending:
                            if cd_ <= 0:
                                f_()
                            else:
                                nxt.append((cd_ - 1, f_))
                        pending = nxt
                        pending.append((LOOK - 1, (lambda bo=bo, vp_=vp_, p_=p_, kn=kn, ks=ks, e_=e_, qs=qs:
                                            mm(ps[bo][0:65, qs:nq], vp_[0:kn, ks, e_ * 66:e_ * 66 + 65], p_[0:kn, qs:nq],
                                               ks == 0, ks == nks - 1, [vp_.b, p_.b], [bps[bo]]))))

                    def epi_a(bo=bo, e_=e_):
                        rd = rden[e_]; to = tO[e_]
                        cp('act', to[0:65, 0:nq], ps[bo][0:65, 0:nq], [bps[bo]], [to.b])
                        P.op('dve', lambda e: e.reciprocal(out=rd[64:65, 0:nq], in_=to[64:65, 0:nq]),
                             reads=[to.b], writes=[rd.b])

                    def epi_b(bo=bo, e_=e_, pr=pr):
                        rd = rden[e_]; to = tO[e_]
                        mm(ps[5][0:64, 0:nq], cn32[64:65, C_ONES, 0:64], rd[64:65, 0:nq], True, True,
                           [cn32.b, rd.b], [bps[5]])
                        tt('dve', Aall[e_ * 64:(e_ + 1) * 64, pr, q0:q0 + nq], to[0:64, 0:nq], ps[5][0:64, 0:nq],
                           ALU.mult, [to.b, bps[5]], [Aall_bs[pr]])
                        if e_ == 1:
                            tt('pool', Aall[:, pr, q0:q0 + nq], Aall[:, pr, q0:q0 + nq], sgate[:, pr, q0:q0 + nq],
                               ALU.mult, [Aall_bs[pr], sgate.b], [Aall_bs[pr]])
                    pending.append((LOOK - 1, epi_a))
                    pending.append((LOOK + 3, epi_b))
            for cd_, f_ in pending:
                f_()
        for pi in range(2):
            s, v = load8(W8_INDEX['bwo'] + pi)
            bb4 = [psG.next() for cc in range(4)]
            if pi == 0:
                for k in range(8):
                    for cc in range(4):
                        mm(ps[bb4[cc]][:, 0:N], v[:, k, cc * 128:(cc + 1) * 128], Aall[:, k, 0:N], k == 0, k == 7,
                           [s.b, Aall_bs[k]], [bps[bb4[cc]]])
            for cc in range(4):
                c = pi * 4 + cc
                b = bb4[cc]
                if pi != 0:
                    for k in range(8):
                        mm(ps[b][:, 0:N], v[:, k, cc * 128:(cc + 1) * 128], Aall[:, k, 0:N], k == 0, k == 7,
                           [s.b, Aall_bs[k]], [bps[b]])
                tt('dve', h[:, c, 0:N], ps[b][:, 0:N], h[:, c, 0:N], ALU.add, [bps[b], h.b], [h.b])
                stat_acc(c, N)

    def convert_caches():
        arena_phase()
        kin = [aview(f"kin{k}", [4, 1024], F32) for k in range(2)]
        vins = [aview(f"vin{k}", [4, 1024], F32) for k in range(2)]
        ktcs = [aview(f"ktc{k}", [8, 512], BF16) for k in range(1)]
        vbcs = [aview(f"vbc{k}", [8, 4, 132], BF16) for k in range(2)]
        Lc = aview("Lc", [32, 16], F32)
        pref = aview("prefc", [32, 16], F32)
        for k in range(2):
            memset('pool', vbcs[k][:, :, :, :], 1.0, [vbcs[k].b])
        for sq_ in range(2):
            dma('pool', Lc[:, :, :], clf[sq_].rearrange("(s p) c -> p s c", p=128), [], [Lc.b], Lc.b.name)
            memset('dve', carry_s[sq_][:, :], 0.0, [carry_s[sq_].b])
            Lf = Lc[:, :, :].rearrange("p a b -> p (a b)")
            mm(ps[4][:, :], cn32[:, C_TRIFULL, :], Lf, True, True, [cn32.b, Lc.b], [bps[4]])
            mm(ps[7][:, :], cn32[:, C_ONES, :], Lf, True, True, [cn32.b, Lc.b], [bps[7]])
            cp('dve', pref[:, 0, :], carry_s[sq_][:, :], [carry_s[sq_].b], [pref.b])
            for i in range(1, 32):
                tt('dve', pref[:, i, :], pref[:, i - 1, :], ps[7][:, (i - 1) * 16:i * 16], ALU.add,
                   [pref.b, bps[7]], [pref.b])
            tt('dve', carry_s[sq_][:, :], pref[:, 31, :], ps[7][:, 31 * 16:32 * 16], ALU.add, [pref.b, bps[7]],
               [carry_s[sq_].b])
            tt('dve', call_s[sq_][:, 0:32, :], ps[4][:, :].rearrange("p (a b) -> p a b", b=16), pref[:, :, :], ALU.add,
               [bps[4], pref.b], [call_s[sq_].b])
            for kt in range(8):
                ki = kin[kt % 2]
                vin = vins[kt % 2]
                dma('sp', ki[:, :, :], ck[sq_, kt * 512:(kt + 1) * 512, :].rearrange("(s p) d -> p s d", p=128),
                    [], [ki.b], ki.b.name)
                dma('sp', vin[:, :, :], cv[sq_, kt * 512:(kt + 1) * 512, :].rearrange("(s p) d -> p s d", p=128),
                    [], [vin.b], vin.b.name)
                ktc = ktcs[0]; vbc = vbcs[kt % 2]
                vb5 = vbc.ap.rearrange("p pr s (e c) -> p pr s e c", e=2)
                for sub in range(4):
                    transposes_to(ki[:, sub, :], lambda pr0: ktc[:, pr0:pr0 + 4, sub * 128:(sub + 1) * 128],
                                  [ki.b], ktc.b)
                    cp(['act', 'dve', 'pool', 'dve'][sub], vb5[:, :, sub, :, 0:64],
                       vin[:, sub, :].rearrange("p (pr e d) -> p pr e d", pr=8, e=2), [vin.b], [vbc.b])
                dma('pool', KTs[sq_][:, :, kt * 512:(kt + 1) * 512].rearrange("pr p t -> p pr t"), ktc[:, :, :],
                    [ktc.b], [bKTs[sq_]], ktc.b.name)
                dma('pool', Vs[sq_][:, :, kt * 4:(kt + 1) * 4, :].rearrange("pr p s c -> p pr s c"), vbc[:, :, :, :],
                    [vbc.b], [bVs[sq_]], vbc.b.name)

    def x_load(tile, sample, xin):
        nsub = 1 if sample else 4
        src = xs if sample else xp[tile * 512:(tile + 1) * 512, :]
        dma('pool', xin[:, 0:nsub, :], src.rearrange("(s p) d -> p s d", p=128), [], [xin.b], xin.b.name)

    def x_prologue(sample, xin):
        N = 128 if sample else 512
        nsub = N // 128
        for c in range(8):
            b = psG.next()
            for sub in range(nsub):
                tr(ps[b][:, sub * 128:(sub + 1) * 128], xin[:, sub, c * 128:(c + 1) * 128], ident32,
                   [xin.b, cn32.b], [bps[b]])
            cp(evac_rr.next(), h[:, c, 0:N], ps[b][:, 0:N], [bps[b]], [h.b])
            stat_acc(c, N)

    def process_tile(tile, sample, prefetch_next=None):
        N = 128 if sample else 512
        nsub = N // 128
        dst = ys if sample else yp[tile * 512:(tile + 1) * 512, :]
        first = (tile == 0 and not sample)
        if sample:
            states = [(Sst[1], Sbf[1]), (Sst[2], Sbf[2])]
        else:
            states = [(Sst[0], Sbf[0])] * 8
        stages = [
            (lambda: (cast_ffn(0, 0, first=True), cast_group('win'), cast_group('awo')), lambda: ffn(N, 0, 0)),
            (lambda: cast_ffn(0, 1), lambda: gla(N, states)),
            (lambda: cast_group('kvf'), lambda: ffn(N, 0, 1)),
            (lambda: cast_ffn(1, 0), lambda: kv(N, tile, sample)),
            (lambda: (cast_group('qg'), cast_group('bwo')), lambda: ffn(N, 1, 0)),
            (lambda: cast_ffn(1, 1), lambda: fox(N, tile, sample)),
            (lambda: None, lambda: ffn(N, 1, 1)),
        ]
        wmode['first'] = first
        for si, (pre, run_) in enumerate(stages):
            if si >= (SSTOP if sample else STOP):
                break
            run_()
        wmode['first'] = False
        arena_phase()
        xin = aview("xout", [4, 1024], F32)
        xnext = None
        if prefetch_next is not None:
            xnext = aview("xinn", [4, 1024], F32)
            x_load(prefetch_next, False, xnext)
        for sub in range(nsub):
            for hf in range(2):
                b = psG.next()
                for i in range(4):
                    c = hf * 4 + i
                    tr(ps[b][:, i * 128:(i + 1) * 128], h[:, c, sub * 128:(sub + 1) * 128], ident32,
                       [h.b, cn32.b], [bps[b]])
                cp(evac_rr.next(), xin[:, sub, hf * 512:(hf + 1) * 512], ps[b][:, :], [bps[b]], [xin.b])
        dma('pool', dst.rearrange("(s p) d -> p s d", p=128), xin[:, 0:nsub, :], [xin.b], [], "xout")
        if xnext is not None:
            x_prologue(False, xnext)

    arena_phase()
    xin0 = aview("xin", [4, 1024], F32)
    x_load(0, False, xin0)
    x_prologue(False, xin0)
    for tile in range(NT):
        process_tile(tile, False, tile + 1 if tile + 1 < NT else None)
    if NT == 8:
        dma('pool', glap.rearrange("h d e -> d h e"), Sst[0][:, :].rearrange("p (h e) -> p h e", e=256),
            [Sst[0].b], [], "S0")
    if do_sample:
        for sq_ in range(2):
            dma('pool', Sst[1 + sq_][:, :].rearrange("p (h e) -> p h e", e=256), sgla[sq_].rearrange("h d e -> d h e"),
                [], [Sst[1 + sq_].b], f"S{1 + sq_}")
            cp('dve', Sbf[1 + sq_][:, :], Sst[1 + sq_][:, :], [Sst[1 + sq_].b], [Sbf[1 + sq_].b])
        if CONV:
            convert_caches()
        else:
            for sq_ in range(2):
                memset('dve', carry_s[sq_][:, :], 0.0, [carry_s[sq_].b])
        arena_phase()
        xin0 = aview("xin", [4, 1024], F32)
        x_load(0, True, xin0)
        x_prologue(True, xin0)
        process_tile(0, True)
        for sq_ in range(2):
            dma('pool', glas[sq_].rearrange("h d e -> d h e"), Sst[1 + sq_][:, :].rearrange("p (h e) -> p h e", e=256),
                [Sst[1 + sq_].b], [], f"S{1 + sq_}")
    stats = P.emit()
    if dbg:
        print('sbuf_bytes_remaining', nc.sbuf_bytes_remaining)
    return nc, stats


_CACHE = {}


def kernel(**inp):
    f = lambda a: np.ascontiguousarray(np.asarray(a, dtype=np.float32))
    shared = host_layout(inp)
    x_prompt = f(inp['x_prompt']); x_sample = f(inp['x_sample'])
    state_gla = f(inp['state_gla']); cache_k = f(inp['cache_k']); cache_v = f(inp['cache_v'])
    cache_logf = f(inp['cache_logf'])
    if 'nc' not in _CACHE:
        _CACHE['nc'] = build_program()[0]
    nc = _CACHE['nc']
    in_maps = []
    for c in range(8):
        m = dict(shared)
        m['xp'] = x_prompt[c]
        m['xs'] = x_sample[2 * c:2 * c + 2].reshape(128, D)
        m['sgla'] = state_gla[2 * c:2 * c + 2, 0]
        m['ck'] = cache_k[2 * c:2 * c + 2].reshape(2, T, D)
        m['cv'] = cache_v[2 * c:2 * c + 2].reshape(2, T, D)
        m['clf'] = cache_logf[2 * c:2 * c + 2]
        in_maps.append(m)
    res = run_bass_kernel_spmd(nc, in_maps, core_ids=list(range(8)))
    r = res.results
    g = lambda name: [np.asarray(r[c][name], dtype=np.float32) for c in range(8)]
    y_prompt = np.stack(g('yp'))
    y_sample = np.concatenate([a.reshape(2, 64, D) for a in g('ys')])
    gla_prompt = np.stack(g('glap'))[:, None]
    gla_sample = np.concatenate(g('glas'))[:, None]
    k_prompt = np.stack(g('kp')).reshape(8, T, 16, 64)
    v_prompt = np.stack(g('vp')).reshape(8, T, 16, 64)
    lf_prompt = np.stack(g('lfp'))
    k_sample = np.concatenate([a.reshape(2, 64, 16, 64) for a in g('kso')])
    v_sample = np.concatenate([a.reshape(2, 64, 16, 64) for a in g('vso')])
    lf_sample = np.concatenate([a.reshape(2, 64, 16) for a in g('lfs')])
    return (y_prompt, y_sample, gla_prompt, gla_sample, k_prompt, v_prompt, lf_prompt, k_sample, v_sample, lf_sample)
```

```python
import contextlib
import numpy as np
import concourse.bass as bass
import concourse.mybir as mybir
from concourse.bass_utils import run_bass_kernel_spmd

F32 = mybir.dt.float32
BF16 = mybir.dt.bfloat16
AF = mybir.ActivationFunctionType
ALU = mybir.AluOpType
AX = mybir.AxisListType

ENGS = ('pe', 'act', 'dve', 'pool', 'sp')

D = 1024
DFF = 2816
T = 4096
EPS = 1e-6
NEG = -30000.0


class Buf:
    __slots__ = ('name', 'last_w', 'readers', 'excl')

    def __init__(self, name, excl=False):
        self.name = name
        self.last_w = None
        self.readers = []
        self.excl = excl


class Op:
    __slots__ = ('eng', 'fn', 'deps', 'is_dma', 'dkey', 'dwaits', 'sig', 'ordv', 'idx', 'label')


class Prog:
    def __init__(self, nc):
        self.nc = nc
        self.ops = []
        self.dcnt = {}
        self.dfence = {}
        self.label = ''

    def buf(self, name, excl=False):
        return Buf(name, excl)

    def op(self, eng, fn, reads=(), writes=(), dma=None):
        o = Op()
        o.eng = eng
        o.fn = fn
        o.is_dma = dma is not None
        o.dkey = dma
        o.idx = len(self.ops)
        o.sig = False
        o.ordv = 0
        o.label = self.label
        deps = set()
        for b in reads:
            if b.last_w is not None:
                deps.add(b.last_w)
            if b.excl:
                for r in b.readers:
                    if self.ops[r].eng != eng:
                        deps.add(r)
        for b in writes:
            if b.last_w is not None:
                deps.add(b.last_w)
            deps.update(b.readers)
        best = {}
        for d in deps:
            p = self.ops[d]
            key = ('d', p.dkey) if p.is_dma else ('e', p.eng)
            if best.get(key, -1) < d:
                best[key] = d
        deps = set(best.values())
        o.deps = deps
        dw = {}
        for d in deps:
            p = self.ops[d]
            if p.is_dma:
                k = p.dkey
                v = self.dcnt[k]
                if dw.get(k, 0) < v:
                    dw[k] = v
                if self.dfence.get(k, 0) < v:
                    self.dfence[k] = v
        if o.is_dma:
            k = o.dkey
            f = self.dfence.get(k, 0)
            if f > 0 and dw.get(k, 0) < f:
                dw[k] = f
            self.dcnt[k] = self.dcnt.get(k, 0) + 1
        o.dwaits = dw
        for b in reads:
            b.readers.append(o.idx)
        for b in writes:
            b.last_w = o.idx
            b.readers = []
        self.ops.append(o)
        return o

    def frontier(self, bufs):
        best = {}
        for b in bufs:
            cand = list(b.readers)
            if b.last_w is not None:
                cand.append(b.last_w)
            for i in cand:
                p = self.ops[i]
                key = ('d', p.dkey) if p.is_dma else ('e', p.eng)
                if best.get(key, -1) < i:
                    best[key] = i
        return list(best.values())

    def emit(self, final_wait_eng='sp'):
        nc = self.nc
        ops = self.ops
        for o in ops:
            for d in o.deps:
                p = ops[d]
                if p.is_dma:
                    continue
                if p.eng == 'pe' and o.eng == 'pe' and not o.is_dma:
                    continue
                p.sig = True
        cnt = {e: 0 for e in ENGS}
        for o in ops:
            if not o.is_dma and o.sig:
                cnt[o.eng] += 1
                o.ordv = cnt[o.eng]
        with contextlib.ExitStack() as st:
            esem = {e: st.enter_context(nc.semaphore('s_' + e)) for e in ENGS if e != 'sp'}
            dsem = {k: st.enter_context(nc.semaphore('d_' + k)) for k in self.dcnt}
            block = st.enter_context(nc.Block())
            per = {e: [o for o in ops if o.eng == e] for e in ENGS}
            dtot = dict(self.dcnt)

            def run(eng_name, eng):
                known = {}
                for o in per[eng_name]:
                    need = {}
                    for d in o.deps:
                        p = ops[d]
                        if p.is_dma:
                            continue
                        if p.eng == 'pe' and eng_name == 'pe' and not o.is_dma:
                            continue
                        key = ('e', p.eng)
                        if need.get(key, 0) < p.ordv:
                            need[key] = p.ordv
                    for k, v in o.dwaits.items():
                        need[('d', k)] = v * 16
                    for key, v in need.items():
                        if known.get(key, 0) >= v:
                            continue
                        s = esem[key[1]] if key[0] == 'e' else dsem[key[1]]
                        eng.wait_ge(s, v)
                        known[key] = v
                    ins = o.fn(eng)
                    if o.is_dma:
                        ins.then_inc(dsem[o.dkey], 16)
                    elif o.sig:
                        ins.then_inc(esem[o.eng], 1)
                if eng_name == final_wait_eng:
                    for k, v in dtot.items():
                        eng.wait_ge(dsem[k], v * 16)
                    for e, c in cnt.items():
                        if e != 'sp' and c > 0:
                            eng.wait_ge(esem[e], c)

            block.tensor(lambda e: run('pe', e))
            block.scalar(lambda e: run('act', e))
            block.vector(lambda e: run('dve', e))
            block.gpsimd(lambda e: run('pool', e))
            block.sync(lambda e: run('sp', e))
        return {e: len(per[e]) for e in ENGS}, cnt


class RR:
    def __init__(self, items):
        self.items = list(items)
        self.i = 0

    def next(self):
        x = self.items[self.i % len(self.items)]
        self.i += 1
        return x


W8_INDEX = {}
_n = 0
for _l in range(2):
    for _i in range(2):
        W8_INDEX[('gu', _l, _i)] = _n
        _n += 11
W8_INDEX['win'] = _n; _n += 6
W8_INDEX['awo'] = _n; _n += 2
W8_INDEX['kvf'] = _n; _n += 4
W8_INDEX['qg'] = _n; _n += 4
W8_INDEX['bwo'] = _n; _n += 2
NP8 = _n
NPD = 32

SP_FFN = 0
SP_MIX = 32
SP_KVN = 48
SP_GOUT = 56
SP_GK = 58
SP_GQ = 122
SP_BF = 186
SP_GK2 = 202
SP_GQ2 = 203
NSP = 204

C_ID, C_TRIBLK, C_LOWBLK, C_TRIFULL, C_ONES, C_SEL127, C_SEL63, C_MASKNEG = range(8)


def _pieces8(w):
    n = w.shape[1] // 512
    return np.ascontiguousarray(w.reshape(8, 128, n, 512).transpose(2, 1, 0, 3))


def host_layout(inp):
    f = lambda a: np.asarray(a, dtype=np.float32)
    W8 = np.empty((NP8, 128, 8, 512), np.float32)
    WD = np.empty((NPD, 128, 22, 128), np.float32)
    wgu = f(inp['w_ffn_gu'])
    wdn = f(inp['w_ffn_down'])
    for l in range(2):
        for i in range(2):
            w = wgu[l, i].reshape(8, 128, 2, 11, 2, 128)
            W8[W8_INDEX[('gu', l, i)]:W8_INDEX[('gu', l, i)] + 11] = \
                w.transpose(3, 1, 0, 4, 2, 5).reshape(11, 128, 8, 512)
            d = wdn[l, i].reshape(22, 128, 8, 128)
            WD[(l * 2 + i) * 8:(l * 2 + i) * 8 + 8] = d.transpose(2, 1, 0, 3)
    win = f(inp['a_w_in'])[0]
    W8[W8_INDEX['win']:W8_INDEX['win'] + 6] = _pieces8(win[:, :3072])
    W8[W8_INDEX['awo']:W8_INDEX['awo'] + 2] = _pieces8(f(inp['a_w_o'])[0])
    wkvf = f(inp['w_kvf'])
    W8[W8_INDEX['kvf']:W8_INDEX['kvf'] + 4] = _pieces8(wkvf[:, :2048])
    W8[W8_INDEX['qg']:W8_INDEX['qg'] + 4] = _pieces8(f(inp['b_w_qg'])[0])
    W8[W8_INDEX['bwo']:W8_INDEX['bwo'] + 2] = _pieces8(f(inp['b_w_o'])[0])
    WS = np.empty((128, 2, 8, 16), np.float32)
    WS[:, 0] = win[:, 3072:3088].reshape(8, 128, 16).transpose(1, 0, 2)
    WS[:, 1] = wkvf[:, 2048:2064].reshape(8, 128, 16).transpose(1, 0, 2)
    WG2 = np.zeros((32, 512), np.float32)
    WG2[0:16] = f(inp['a_w_g2'])[0]
    WG2[16] = f(inp['a_b_g'])[0]
    SP = np.zeros((128, NSP), np.float32)
    fn = f(inp['ffn_norm'])
    for l in range(2):
        for i in range(2):
            SP[:, SP_FFN + (l * 2 + i) * 8:SP_FFN + (l * 2 + i) * 8 + 8] = fn[l, i].reshape(8, 128).T
    mn = f(inp['mix_norm'])
    for l in range(2):
        SP[:, SP_MIX + l * 8:SP_MIX + l * 8 + 8] = mn[l].reshape(8, 128).T
    SP[:, SP_KVN:SP_KVN + 8] = f(inp['kv_norm']).reshape(8, 128).T
    SP[:, SP_GOUT:SP_GOUT + 2] = f(inp['a_g_out'])[0].reshape(2, 128).T
    SP[:, SP_GK:SP_GK + 64] = f(inp['g_k'])[None, :]
    SP[:, SP_GQ:SP_GQ + 64] = f(inp['b_g_q'])[0][None, :]
    SP[:, SP_BF:SP_BF + 16] = f(inp['b_f'])[None, :]
    SP[:, SP_GK2] = np.tile(f(inp['g_k']), 2)
    SP[:, SP_GQ2] = np.tile(f(inp['b_g_q'])[0], 2)
    CN = np.zeros((128, 8, 128), np.float32)
    j = np.arange(128)[:, None]
    t = np.arange(128)[None, :]
    same = (j // 64) == (t // 64)
    CN[:, C_ID] = (j == t)
    CN[:, C_TRIBLK] = same & (j <= t)
    CN[:, C_LOWBLK] = same & (j > t)
    CN[:, C_TRIFULL] = (j <= t)
    CN[:, C_ONES] = 1.0
    CN[:, C_SEL127] = (j == 127) & (t >= 0)
    CN[:, C_SEL63] = (j == 63) & (t >= 0)
    CN[:, C_MASKNEG] = np.where(j <= t, 0.0, NEG)
    SEL = np.zeros((128, 16, 128), np.float32)
    for hh in range(16):
        e = hh % 2
        SEL[e * 64 + hh, hh, :] = 1.0
        SEL[e * 64 + 32 + hh, hh, :] = 1.0
    return dict(W8=W8.reshape(NP8, 128, 4096), WD=WD.reshape(NPD, 128, 2816), WS=WS.reshape(128, 256),
                WG2=WG2, SP=SP, CN=CN.reshape(128, 1024), SEL=SEL.reshape(128, 2048))


def build_program(NT=8, do_sample=True, dbg=False, STOP=99, KVF=255, KVS=9, SSTOP=99, CONV=1):
    nc = bass.Bass("TRN2", target_bir_lowering=False)
    P = Prog(nc)

    def din(name, shape, dt=F32):
        return nc.dram_tensor(name, shape, dt, kind="ExternalInput").ap()

    def dout(name, shape):
        return nc.dram_tensor(name, shape, F32, kind="ExternalOutput").ap()

    def dscr(name, shape, dt=BF16):
        return nc.dram_tensor(name, shape, dt, kind="Internal").ap()

    xp = din("xp", [T, D]); xs = din("xs", [128, D])
    sgla = din("sgla", [2, 4, 128, 256])
    ck = din("ck", [2, T, D]); cv = din("cv", [2, T, D]); clf = din("clf", [2, T, 16])
    W8f = din("W8", [NP8, 128, 4096]); WDf = din("WD", [NPD, 128, 2816])
    WSf = din("WS", [128, 256]); WG2f = din("WG2", [32, 512]); SPf = din("SP", [128, NSP]); CNf = din("CN", [128, 1024]); SELf = din("SEL", [128, 2048])
    yp = dout("yp", [T, D]); ys = dout("ys", [128, D])
    glap = dout("glap", [4, 128, 256]); glas = dout("glas", [2, 4, 128, 256])
    kp = dout("kp", [T, D]); vp = dout("vp", [T, D]); lfp = dout("lfp", [T, 16])
    kso = dout("kso", [128, D]); vso = dout("vso", [128, D]); lfs = dout("lfs", [128, 16])
    W8b = dscr("scr_w8", [NP8, 128, 4096]); WDb = dscr("scr_wd", [NPD, 128, 2816])
    KTp = dscr("scr_kt_p", [8, 128, T]); Vp = dscr("scr_v_p", [8, 128, 32, 132])
    KTs = [dscr(f"scr_kt_s{i}", [8, 128, T + 64]) for i in range(2)]
    Vs = [dscr(f"scr_v_s{i}", [8, 128, 33, 132]) for i in range(2)]
    bW8 = [P.buf(f"W8b{i}") for i in range(NP8)]
    bWD = [P.buf(f"WDb{i}") for i in range(NPD)]
    bKTp = P.buf("KTp"); bVp = P.buf("Vp")
    bKTs = [P.buf("KTs0"), P.buf("KTs1")]; bVs = [P.buf("Vs0"), P.buf("Vs1")]

    class TB:
        def __init__(self, name, shape, dt=F32):
            self.t = nc.alloc_sbuf_tensor(name, shape, dt)
            self.b = P.buf(name)
            self.name = name

        def __getitem__(self, k):
            return self.t[k]

    NSLOT = 4
    ws = [TB(f"ws{i}", [128, 4096], BF16) for i in range(NSLOT)]
    ws_rr = RR(range(NSLOT))
    h = TB("h", [128, 8, 512])
    xn = TB("xn", [128, 8, 512], BF16)
    xn.bs = [P.buf(f"xn{c_}") for c_ in range(8)]
    sq = TB("sq", [128, 8, 512], BF16)
    sq.bs = [P.buf(f"sq{c_}") for c_ in range(8)]
    lnv = TB("lnv", [128, 512]); rstd = TB("rstd", [128, 512])
    sel2 = TB("sel2", [128, 16, 128], BF16)
    Sst = [TB(f"S{i}", [128, 1024]) for i in range(3)]
    Sbf = [TB(f"Sbf{i}", [128, 1024], BF16) for i in range(3)]
    call_p = TB("call_p", [128, 32, 16])
    call_s = [TB(f"call_s{i}", [128, 33, 16]) for i in range(2)]
    carry_p = TB("carry_p", [128, 16])
    carry_s = [TB(f"carry_s{i}", [128, 16]) for i in range(2)]
    cn32 = TB("cn32", [128, 8, 128])
    idbf = TB("idbf", [128, 128], BF16); onesbf = TB("onesbf", [128, 128], BF16)
    mblkbf = TB("mblkbf", [128, 128], BF16); mnegbf = TB("mnegbf", [128, 128], BF16)
    spar = TB("spar", [128, NSP])
    wsm32 = TB("wsm32", [128, 256]); wsm = TB("wsm", [128, 2, 8, 16], BF16)
    wg2_32 = TB("wg2_32", [32, 512]); wg2 = TB("wg2", [32, 512], BF16)
    ARENA_B = 100 * 1024
    arena = nc.alloc_sbuf_tensor("arena", [128, ARENA_B // 2], BF16)
    ps = [nc.alloc_psum_tensor(f"ps{i}", [128, 512], F32) for i in range(8)]
    bps = [P.buf(f"ps{i}", excl=True) for i in range(8)]

    class AV:
        def __init__(self, ap, b):
            self.ap = ap
            self.b = b

        def __getitem__(self, k):
            return self.ap[k]

    arena_state = {'bufs': [], 'off': 0, 'front': []}

    def arena_phase():
        arena_state['front'] = P.frontier(arena_state['bufs'])
        arena_state['bufs'] = []
        arena_state['off'] = 0

    def aview(name, free_shape, dt, parts=128):
        esz = 4 if dt == F32 else 2
        n = int(np.prod(free_shape))
        nb = n * esz
        off = (arena_state['off'] + 31) // 32 * 32
        assert off + nb <= ARENA_B, (name, off, nb)
        arena_state['off'] = off + nb
        ap = arena[0:parts, off // 2:(off + nb) // 2]
        if dt == F32:
            ap = ap.bitcast(F32)
        if len(free_shape) == 2:
            ap = ap.rearrange("p (a b) -> p a b", b=free_shape[1])
        elif len(free_shape) == 3:
            ap = ap.rearrange("p (a b c) -> p a b c", b=free_shape[1], c=free_shape[2])
        b = P.buf(name)
        b.readers = list(arena_state['front'])
        arena_state['bufs'].append(b)
        return AV(ap, b)

    def mm(out, lhsT, rhs, start, stop, reads, writes):
        P.op('pe', lambda e: e.matmul(out, lhsT=lhsT, rhs=rhs, start=start, stop=stop, skip_group_check=True),
             reads=reads, writes=writes)

    def tr(out, in_, ident, reads, writes):
        P.op('pe', lambda e: e.transpose(out, in_, ident), reads=reads, writes=writes)

    def act(out, in_, func, reads, writes, bias=None, scale=1.0):
        if bias is None:
            P.op('act', lambda e: e.activation(out=out, in_=in_, func=func, scale=scale), reads=reads, writes=writes)
        else:
            P.op('act', lambda e: e.activation(out=out, in_=in_, func=func, bias=bias, scale=scale),
                 reads=reads, writes=writes)

    def cp(eng, out, in_, reads, writes):
        if eng == 'act':
            P.op('act', lambda e: e.copy(out=out, in_=in_), reads=reads, writes=writes)
        else:
            P.op(eng, lambda e: e.tensor_copy(out=out, in_=in_), reads=reads, writes=writes)

    def tt(eng, out, in0, in1, op, reads, writes):
        P.op(eng, lambda e: e.tensor_tensor(out=out, in0=in0, in1=in1, op=op), reads=reads, writes=writes)

    def stt(eng, out, in0, scalar, in1, op0, op1, reads, writes):
        P.op(eng, lambda e: e.scalar_tensor_tensor(out=out, in0=in0, scalar=scalar, in1=in1, op0=op0, op1=op1),
             reads=reads, writes=writes)

    def ts(eng, out, in0, s1, op0, reads, writes, s2=None, op1=None):
        if op1 is None:
            P.op(eng, lambda e: e.tensor_scalar(out=out, in0=in0, scalar1=s1, scalar2=None, op0=op0),
                 reads=reads, writes=writes)
        else:
            P.op(eng, lambda e: e.tensor_scalar(out=out, in0=in0, scalar1=s1, scalar2=s2, op0=op0, op1=op1),
                 reads=reads, writes=writes)

    def dma(q, out, in_, reads, writes, key):
        P.op(q, lambda e: e.dma_start(out=out, in_=in_), reads=reads, writes=writes, dma=key)

    def memset(eng, ap, val, writes):
        P.op(eng, lambda e: e.memset(ap, val), writes=writes)

    dma('sp', cn32[:].rearrange("p a b -> p (a b)"), CNf, [], [cn32.b], "cn32")
    dma('sp', spar[:], SPf, [], [spar.b], "spar")
    dma('sp', wsm32[:], WSf, [], [wsm32.b], "wsm32")
    dma('sp', wg2_32[:], WG2f, [], [wg2_32.b], "wg2_32")
    cp('dve', idbf[:], cn32[:, C_ID, :], [cn32.b], [idbf.b])
    cp('dve', onesbf[:], cn32[:, C_ONES, :], [cn32.b], [onesbf.b])
    cp('dve', mblkbf[:], cn32[:, C_TRIBLK, :], [cn32.b], [mblkbf.b])
    cp('dve', mnegbf[:], cn32[:, C_MASKNEG, :], [cn32.b], [mnegbf.b])
    cp('dve', wsm[:].rearrange("p a b c -> p (a b c)"), wsm32[:], [wsm32.b], [wsm.b])
    cp('dve', wg2[:], wg2_32[:], [wg2_32.b], [wg2.b])
    memset('dve', Sst[0][:], 0.0, [Sst[0].b])
    memset('dve', Sbf[0][:], 0.0, [Sbf[0].b])
    memset('dve', carry_p[:], 0.0, [carry_p.b])
    for i_ in range(2):
        memset('dve', call_s[i_][:, 32, :], 0.0, [call_s[i_].b])

    ident32 = cn32[:, C_ID, :]
    arena_phase()
    seltmp = aview("seltmp", [2048], F32)
    dma('sp', seltmp[:, :], SELf, [], [seltmp.b], "seltmp")
    cp('dve', sel2[:].rearrange("p a b -> p (a b)"), seltmp[:, :], [seltmp.b], [sel2.b])

    def cast8(i, key):
        dma('pool', W8b[i], W8f[i], [], [bW8[i]], key)

    def castd(i, key):
        dma('pool', WDb[i], WDf[i], [], [bWD[i]], key)

    def cast_ffn(l, i, first=False):
        base = W8_INDEX[('gu', l, i)]
        for j in range(11):
            key = f"cg{l}{i}"
            if first:
                key = "cgA0" if j < 1 else ("cgA1" if j < 3 else ("cgA2" if j < 6 else "cgA3"))
            cast8(base + j, key)
        for c in range(8):
            castd((l * 2 + i) * 8 + c, f"cd{l}{i}")

    def cast_group(name):
        n = {'win': 6, 'awo': 2, 'kvf': 4, 'qg': 4, 'bwo': 2}[name]
        for j in range(n):
            cast8(W8_INDEX[name] + j, "c" + name)

    wmode = {'first': False}

    def load8(idx):
        s = ws[ws_rr.next()]
        if wmode['first']:
            dma('pool', s[:, :], W8f[idx], [], [s.b], s.name)
            dma('sp', W8b[idx], s[:, :], [s.b], [bW8[idx]], s.name + "st")
        else:
            dma('sp', s[:, :], W8b[idx], [bW8[idx]], [s.b], s.name)
        return s, s[:, :].rearrange("p (k w) -> p k w", w=512)

    def loadd(idx):
        s = ws[ws_rr.next()]
        if wmode['first']:
            dma('pool', s[:, 0:2816], WDf[idx], [], [s.b], s.name)
            dma('sp', WDb[idx], s[:, 0:2816], [s.b], [bWD[idx]], s.name + "st")
        else:
            dma('sp', s[:, 0:2816], WDb[idx], [bWD[idx]], [s.b], s.name)
        return s, s[:, 0:2816].rearrange("p (m w) -> p m w", w=128)

    psG = RR([0, 1, 2, 3])
    evac_rr = RR(['act', 'dve'])

    stat_pend = []

    def stat_acc(c, N):
        act(sq[:, c, 0:N], h[:, c, 0:N], AF.Square, [h.b], [sq.bs[c]])
        for f_ in stat_pend:
            f_()
        del stat_pend[:]
        stat_pend.append(lambda: mm(ps[7][:, 0:N], onesbf[:], sq[:, c, 0:N], c == 0, c == 7,
                                    [onesbf.b, sq.bs[c]], [bps[7]]))
        if c == 7:
            for f_ in stat_pend:
                f_()
            del stat_pend[:]

    def norm(N, gcol, reuse=False):
        if not reuse:
            act(lnv[:, 0:N], ps[7][:, 0:N], AF.Ln, [bps[7]], [lnv.b], bias=EPS, scale=1.0 / D)
            act(rstd[:, 0:N], lnv[:, 0:N], AF.Exp, [lnv.b], [rstd.b], scale=-0.5)
        for c in range(8):
            stt('dve', xn[:, c, 0:N], h[:, c, 0:N], spar[:, gcol + c:gcol + c + 1], rstd[:, 0:N],
                ALU.mult, ALU.mult, [h.b, spar.b, rstd.b], [xn.bs[c]])

    def ffn(N, l, i):
        norm(N, SP_FFN + (l * 2 + i) * 8, reuse=(l == 1 and i == 0))
        arena_phase()
        hid = aview("hid", [22, 512], BF16)
        hid_bs = []
        for m_ in range(22):
            b_ = P.buf(f"hid{m_}")
            b_.readers = list(hid.b.readers)
            arena_state['bufs'].append(b_)
            hid_bs.append(b_)
        sg = [aview(f"sg{k}", [512], BF16) for k in range(2)]
        sg_rr = RR([0, 1])
        base = W8_INDEX[('gu', l, i)]
        for j in range(11):
            s, v = load8(base + j)
            banks = [(psG.next(), psG.next()) for mi in range(2)]
            if j == 0:
                for k in range(8):
                    for mi in range(2):
                        for gu in range(2):
                            mm(ps[banks[mi][gu]][:, 0:N], v[:, k, (mi * 2 + gu) * 128:(mi * 2 + gu + 1) * 128],
                               xn[:, k, 0:N], k == 0, k == 7, [s.b, xn.bs[k]], [bps[banks[mi][gu]]])
            for mi in range(2):
                m = 2 * j + mi
                bg, bu = banks[mi]
                if j != 0:
                    for k in range(8):
                        mm(ps[bg][:, 0:N], v[:, k, (mi * 2) * 128:(mi * 2 + 1) * 128], xn[:, k, 0:N], k == 0, k == 7,
                           [s.b, xn.bs[k]], [bps[bg]])
                    for k in range(8):
                        mm(ps[bu][:, 0:N], v[:, k, (mi * 2 + 1) * 128:(mi * 2 + 2) * 128], xn[:, k, 0:N], k == 0, k == 7,
                           [s.b, xn.bs[k]], [bps[bu]])
                g = sg[sg_rr.next()]
                act(g[:, 0:N], ps[bg][:, 0:N], AF.Silu, [bps[bg]], [g.b])
                tt('dve', hid[:, m, 0:N], g[:, 0:N], ps[bu][:, 0:N], ALU.mult, [g.b, bps[bu]], [hid_bs[m]])
        for c in range(8):
            s, v = loadd((l * 2 + i) * 8 + c)
            b = psG.next()
            for m in range(22):
                mm(ps[b][:, 0:N], v[:, m, :], hid[:, m, 0:N], m == 0, m == 21, [s.b, hid_bs[m]], [bps[b]])
            stt('dve', h[:, c, 0:N], ps[b][:, 0:N], 0.5, h[:, c, 0:N], ALU.mult, ALU.add, [bps[b], h.b], [h.b])
            stat_acc(c, N)

    def gla(N, chunk_states):
        nsub = N // 128
        nch = N // 64
        norm(N, SP_MIX + 0)
        arena_phase()
        qk = aview("qk", [8, 512], BF16)
        silur = aview("silur", [8, 512], BF16)
        ktm = aview("ktm", [4, 512], BF16)
        vtm = aview("vtm", [4, 1024], BF16)
        sptm = aview("sptm", [4, 512], F32)
        etmp = aview("etmp", [512], F32)
        e1 = [aview(f"e1_{k}", [512], F32) for k in range(2)]
        e2 = [aview(f"e2_{k}", [512], F32) for k in range(2)]
        fb = [aview(f"fb{k}", [512], F32) for k in range(2)]
        am = aview("am", [4, 512], BF16)
        osb = aview("osb", [8, 512], F32)
        glaug = aview("glaug", [512], BF16, parts=32)
        dec = aview("dec", [4, 8], F32)
        rsh = [aview(f"rsh{k}", [512], F32) for k in range(2)]
        wi = W8_INDEX['win']
        for pi in range(2):
            s, v = load8(wi + pi)
            bb4 = [psG.next() for hh in range(4)]
            if pi == 0:
                for k in range(8):
                    for hh in range(4):
                        mm(ps[bb4[hh]][:, 0:N], v[:, k, hh * 128:(hh + 1) * 128], xn[:, k, 0:N], k == 0, k == 7,
                           [s.b, xn.bs[k]], [bps[bb4[hh]]])
            for hh in range(4):
                b = bb4[hh]
                if pi != 0:
                    for k in range(8):
                        mm(ps[b][:, 0:N], v[:, k, hh * 128:(hh + 1) * 128], xn[:, k, 0:N], k == 0, k == 7,
                           [s.b, xn.bs[k]], [bps[b]])
                cp(evac_rr.next(), qk[:, pi * 4 + hh, 0:N], ps[b][:, 0:N], [bps[b]], [qk.b])
            if pi == 1:
                for sub in range(nsub):
                    b = psG.next()
                    for k in range(8):
                        mm(ps[b][:, :], xn[:, k, sub * 128:(sub + 1) * 128], v[:, k, :], k == 0, k == 7,
                           [s.b, xn.bs[k]], [bps[b]])
                    cp(evac_rr.next(), ktm[:, sub, :], ps[b][:, :], [bps[b]], [ktm.b])
        for pi in range(2):
            s, v = load8(wi + 2 + pi)
            for sub in range(nsub):
                b = psG.next()
                for k in range(8):
                    mm(ps[b][:, :], xn[:, k, sub * 128:(sub + 1) * 128], v[:, k, :], k == 0, k == 7,
                       [s.b, xn.bs[k]], [bps[b]])
                cp(evac_rr.next(), vtm[:, sub, pi * 512:(pi + 1) * 512], ps[b][:, :], [bps[b]], [vtm.b])
        r_state = {}

        def emit_r_groups(ch):
            gpc = 8 // nch
            for gi in range(ch * gpc, (ch + 1) * gpc):
                pi, hh = gi // 4, gi % 4
                if pi not in r_state:
                    r_state[pi] = load8(wi + 4 + pi)
                s, v = r_state[pi]
                b = psG.next()
                for k in range(8):
                    mm(ps[b][:, 0:N], v[:, k, hh * 128:(hh + 1) * 128], xn[:, k, 0:N], k == 0, k == 7,
                       [s.b, xn.bs[k]], [bps[b]])
                act(silur[:, gi, 0:N], ps[b][:, 0:N], AF.Silu, [bps[b]], [silur.b])
        memset('dve', glaug[0:32, 0:N], 1.0, [glaug.b])
        b = psG.next()
        for k in range(8):
            mm(ps[b][0:16, 0:N], wsm[:, 0, k, :], xn[:, k, 0:N], k == 0, k == 7, [wsm.b, xn.bs[k]], [bps[b]])
        cp('dve', glaug[0:16, 0:N], ps[b][0:16, 0:N], [bps[b]], [glaug.b])
        for sub in range(nsub):
            b = psG.next()
            mm(ps[b][:, :], glaug[0:32, sub * 128:(sub + 1) * 128], wg2[0:32, :], True, True, [glaug.b, wg2.b], [bps[b]])
            act(etmp[:, :], ps[b][:, :], AF.Exp, [bps[b]], [etmp.b], scale=-1.0)
            act(sptm[:, sub, :], etmp[:, :], AF.Ln, [etmp.b], [sptm.b], bias=1.0)
        for hh in range(4):
            b = psG.next()
            for sub in range(nsub):
                mm(ps[b][:, sub * 128:(sub + 1) * 128], sptm[:, sub, hh * 128:(hh + 1) * 128], cn32[:, C_TRIBLK, :],
                   True, True, [sptm.b, cn32.b], [bps[b]])
            a1 = e1[hh % 2]
            a2 = e2[hh % 2]
            act(a1[:, 0:N], ps[b][:, 0:N], AF.Exp, [bps[b]], [a1.b], scale=-1.0 / 16.0)
            act(a2[:, 0:N], ps[b][:, 0:N], AF.Exp, [bps[b]], [a2.b], scale=1.0 / 16.0)
            stt('dve', qk[:, hh, 0:N], qk[:, hh, 0:N], 128.0 ** -0.5, a1[:, 0:N], ALU.mult, ALU.mult,
                [qk.b, a1.b], [qk.b])
            tt('pool', qk[:, 4 + hh, 0:N], qk[:, 4 + hh, 0:N], a2[:, 0:N], ALU.mult, [qk.b, a2.b], [qk.b])
            cp('dve', dec[:, hh, 0:nch], a1[:, 0:N].rearrange("p (c t) -> p c t", t=64)[:, :, 63], [a1.b], [dec.b])
        for sub in range(nsub):
            b = psG.next()
            mm(ps[b][:, :], cn32[:, C_LOWBLK, :], sptm[:, sub, :], True, True, [cn32.b, sptm.b], [bps[b]])
            f_ = fb[sub % 2]
            act(f_[:, :], ps[b][:, :], AF.Exp, [bps[b]], [f_.b], scale=-1.0 / 16.0)
            tt('dve', ktm[:, sub, :], ktm[:, sub, :], f_[:, :], ALU.mult, [ktm.b, f_.b], [ktm.b])
        for hh in range(4):
            b = psG.next()
            for sub in range(nsub):
                mm(ps[b][:, sub * 128:(sub + 1) * 128], qk[:, 4 + hh, sub * 128:(sub + 1) * 128],
                   qk[:, hh, sub * 128:(sub + 1) * 128], True, True, [qk.b], [bps[b]])
            tt('dve', am[:, hh, 0:N].rearrange("p (s t) -> p s t", t=128),
               ps[b][:, 0:N].rearrange("p (s t) -> p s t", t=128),
               mblkbf[:].unsqueeze(1).broadcast_to([128, nsub, 128]), ALU.mult, [bps[b], mblkbf.b], [am.b])
        psO = RR([4, 5])
        for ch in range(nch):
            sub = ch // 2
            r0 = (ch % 2) * 64
            S, Sb = chunk_states[ch]
            bo = psO.next()
            for hh in range(4):
                for dvc in range(2):
                    col = (hh * 2 + dvc) * 64
                    mm(ps[bo][:, col:col + 64], Sb[:, hh * 256 + dvc * 128:hh * 256 + (dvc + 1) * 128],
                       qk[:, hh, ch * 64:(ch + 1) * 64], True, False, [Sb.b, qk.b], [bps[bo]])
                    mm(ps[bo][:, col:col + 64], vtm[:, sub, hh * 256 + dvc * 128:hh * 256 + (dvc + 1) * 128],
                       am[:, hh, sub * 128 + r0:sub * 128 + r0 + 64], False, True, [vtm.b, am.b], [bps[bo]])
            cp('act', osb[:, :, ch * 64:(ch + 1) * 64], ps[bo][:, :].rearrange("p (a t) -> p a t", t=64),
               [bps[bo]], [osb.b])
            for hh in range(4):
                bd = 6 + hh // 2
                c0 = (hh % 2) * 256
                mm(ps[bd][:, c0:c0 + 256], ktm[r0:r0 + 64, sub, hh * 128:(hh + 1) * 128],
                   vtm[r0:r0 + 64, sub, hh * 256:(hh + 1) * 256], True, True, [ktm.b, vtm.b], [bps[bd]])
            for hh in range(4):
                bd = 6 + hh // 2
                c0 = (hh % 2) * 256
                stt('dve', S[:, hh * 256:(hh + 1) * 256], S[:, hh * 256:(hh + 1) * 256], dec[:, hh, ch:ch + 1],
                    ps[bd][:, c0:c0 + 256], ALU.mult, ALU.add, [S.b, dec.b, bps[bd]], [S.b])
            cp('act', Sb[:, :], S[:, :], [S.b], [Sb.b])
            emit_r_groups(ch)
        tq_bs = []
        qk_front = P.frontier([qk.b])
        for c in range(8):
            b_ = P.buf(f"tq{c}")
            b_.readers = list(qk_front)
            arena_state['bufs'].append(b_)
            tq_bs.append(b_)
        for c in range(8):
            tt('pool' if c % 2 == 0 else 'dve', qk[:, c, 0:N], osb[:, c, 0:N], silur[:, c, 0:N], ALU.mult,
               [osb.b, silur.b], [tq_bs[c]])
        for hh in range(4):
            act(sq[:, 2 * hh:2 * hh + 2, 0:N], osb[:, 2 * hh:2 * hh + 2, 0:N], AF.Square, [osb.b],
                [sq.bs[2 * hh], sq.bs[2 * hh + 1]])
            b = psG.next()
            for dvc in range(2):
                mm(ps[b][:, 0:N], onesbf[:], sq[:, hh * 2 + dvc, 0:N], dvc == 0, dvc == 1,
                   [onesbf.b, sq.bs[hh * 2 + dvc]], [bps[b]])
            act(lnv[:, 0:N], ps[b][:, 0:N], AF.Ln, [bps[b]], [lnv.b], bias=EPS, scale=1.0 / 256.0)
            r_ = rsh[hh % 2]
            act(r_[:, 0:N], lnv[:, 0:N], AF.Exp, [lnv.b], [r_.b], scale=-0.5)
            for dvc in range(2):
                c = hh * 2 + dvc
                stt('dve', xn[:, c, 0:N], qk[:, c, 0:N], spar[:, SP_GOUT + dvc:SP_GOUT + dvc + 1], r_[:, 0:N],
                    ALU.mult, ALU.mult, [tq_bs[c], spar.b, r_.b], [xn.bs[c]])
        for pi in range(2):
            s, v = load8(W8_INDEX['awo'] + pi)
            bb4 = [psG.next() for cc in range(4)]
            if pi == 0:
                for k in range(8):
                    for cc in range(4):
                        mm(ps[bb4[cc]][:, 0:N], v[:, k, cc * 128:(cc + 1) * 128], xn[:, k, 0:N], k == 0, k == 7,
                           [s.b, xn.bs[k]], [bps[bb4[cc]]])
            for cc in range(4):
                c = pi * 4 + cc
                b = bb4[cc]
                if pi != 0:
                    for k in range(8):
                        mm(ps[b][:, 0:N], v[:, k, cc * 128:(cc + 1) * 128], xn[:, k, 0:N], k == 0, k == 7,
                           [s.b, xn.bs[k]], [bps[b]])
                tt('dve', h[:, c, 0:N], ps[b][:, 0:N], h[:, c, 0:N], ALU.add, [bps[b], h.b], [h.b])
                stat_acc(c, N)

    def headnorm(b0, b1, gcol, raw, sqt, ssk, lnk, rk, out=None, rows=128):
        R = slice(0, rows)
        act(sqt[R, 0:512], ps[b0][R, :], AF.Square, [bps[b0]], [sqt.b])
        act(sqt[R, 512:1024], ps[b1][R, :], AF.Square, [bps[b1]], [sqt.b])
        P.op('dve', lambda e: e.tensor_reduce(out=ssk[R, :], in_=sqt[R, :].rearrange("p (a b) -> p a b", b=64),
                                              axis=AX.X, op=ALU.add), reads=[sqt.b], writes=[ssk.b])
        act(lnk[R, :], ssk[R, :], AF.Ln, [ssk.b], [lnk.b], bias=EPS, scale=1.0 / 64.0)
        act(rk[R, :], lnk[R, :], AF.Exp, [lnk.b], [rk.b], scale=-0.5)
        for hf, bb in enumerate((b0, b1)):
            tt('dve', raw[R, hf * 512:(hf + 1) * 512].rearrange("p (a b) -> p a b", b=64),
               ps[bb][R, :].rearrange("p (a b) -> p a b", b=64),
               rk[R, hf * 8:(hf + 1) * 8].unsqueeze(2).broadcast_to([rows, 8, 64]), ALU.mult,
               [bps[bb], rk.b], [raw.b])
        if out is not None:
            tt('pool', out[R, :].rearrange("p (a b) -> p a b", b=64), raw[R, :].rearrange("p (a b) -> p a b", b=64),
               spar[R, gcol:gcol + 64].unsqueeze(1).broadcast_to([rows, 16, 64]), ALU.mult, [raw.b, spar.b], [out.b])

    psT = RR([5, 6])

    def transposes_to(src, dst_fn, reads, dst_b, rows=128, gcol=None):
        for half in range(2):
            b = psT.next()
            for i in range(4):
                pr = half * 4 + i
                tr(ps[b][:, i * 128:i * 128 + rows], src[0:rows, pr * 128:(pr + 1) * 128], ident32[0:rows, 0:rows],
                   reads + [cn32.b], [bps[b]])
            src_v = ps[b][:, :].rearrange("p (a t) -> p a t", t=128)[:, :, 0:rows]
            if gcol is None:
                cp(evac_rr.next(), dst_fn(half * 4), src_v, [bps[b]], [dst_b])
            elif half == 0:
                act(dst_fn(half * 4), src_v, AF.Copy, [bps[b], spar.b], [dst_b], scale=spar[:, gcol:gcol + 1])
            else:
                ts('dve', dst_fn(half * 4), src_v, spar[:, gcol:gcol + 1], ALU.mult, [bps[b], spar.b], [dst_b])

    def cumsum_tile(L, nsub, carry, call, ks0, psb, pref, tri=C_TRIFULL):
        n16 = nsub * 16
        Lf = L[:, 0:nsub, :].rearrange("p a b -> p (a b)")
        mm(ps[psb][:, 0:n16], cn32[:, tri, :], Lf, True, True, [cn32.b, L.b], [bps[psb]])
        mm(ps[psb][:, 256:256 + n16], cn32[:, C_ONES, :], Lf, True, True, [cn32.b, L.b], [bps[psb]])
        cp('dve', pref[:, 0, :], carry[:, :], [carry.b], [pref.b])
        for i in range(1, nsub):
            tt('dve', pref[:, i, :], pref[:, i - 1, :], ps[psb][:, 256 + (i - 1) * 16:256 + i * 16], ALU.add,
               [pref.b, bps[psb]], [pref.b])
        tt('dve', carry[:, :], pref[:, nsub - 1, :], ps[psb][:, 256 + (nsub - 1) * 16:256 + nsub * 16], ALU.add,
           [pref.b, bps[psb]], [carry.b])
        tt('dve', call[:, ks0:ks0 + nsub, :], ps[psb][:, 0:n16].rearrange("p (a b) -> p a b", b=16),
           pref[:, 0:nsub, :], ALU.add, [bps[psb], pref.b], [call.b])

    def kv(N, tile, sample):
        nsub = N // 128
        norm(N, SP_KVN)
        arena_phase()
        kout = [aview(f"kout{k}", [1024], F32) for k in range(2)]
        vout = [aview(f"vout{k}", [1024], F32) for k in range(2)]
        sqt = aview("sqt", [1024], F32)
        kraw = aview("kraw", [1024], F32)
        vbt = aview("vbt", [8, nsub, 132], BF16)
        vb5 = vbt.ap.rearrange("p pr s (e c) -> p pr s e c", e=2)
        ktn = aview("ktn", [8, 512], BF16)
        Lt = aview("Lt", [4, 16], F32)
        ssk = aview("ssk", [16], F32); lnk = aview("lnk", [16], F32); rk = aview("rk", [16], F32)
        zz = aview("zz", [16], F32); ez = aview("ez", [16], F32)
        pref = aview("pref", [4, 16], F32)
        memset('pool', vbt[:, :, :, :], 1.0, [vbt.b])
        wk = W8_INDEX['kvf']
        pieces = [load8(wk + j) for j in range(4)]
        for sub in range(nsub):
            tok0 = tile * 512 + sub * 128
            ko = kout[sub % 2]; vo = vout[sub % 2]
            kb = []
            for hf in range(2):
                b = psG.next()
                s, v = pieces[hf]
                for k in range(8):
                    mm(ps[b][:, :], xn[:, k, sub * 128:(sub + 1) * 128], v[:, k, :], k == 0, k == 7,
                       [s.b, xn.bs[k]], [bps[b]])
                kb.append(b)
            for k in range(8):
                mm(ps[4][:, 0:16], xn[:, k, sub * 128:(sub + 1) * 128], wsm[:, 1, k, :], k == 0, k == 7,
                   [wsm.b, xn.bs[k]], [bps[4]])
            if KVS < 2:
                continue
            headnorm(kb[0], kb[1], SP_GK, kraw, sqt, ssk, lnk, rk, out=ko)
            if KVS < 3:
                continue
            if not sample:
                if KVF & 1:
                    dma('pool', kp[tok0:tok0 + 128, :], ko[:, :], [ko.b], [], ko.b.name)
            else:
                dma('pool', kso[:, :], ko[:, :], [ko.b], [], ko.b.name)
            vbk = []
            for hf in range(2):
                b = psG.next()
                s, v = pieces[2 + hf]
                for k in range(8):
                    mm(ps[b][:, :], xn[:, k, sub * 128:(sub + 1) * 128], v[:, k, :], k == 0, k == 7,
                       [s.b, xn.bs[k]], [bps[b]])
                vbk.append(b)
            transposes_to(kraw, lambda pr0: ktn[:, pr0:pr0 + 4, sub * 128:(sub + 1) * 128], [kraw.b], ktn.b,
                          gcol=SP_GK2)
            for hf in range(2):
                if KVF & 64:
                    cp('act', vo[:, hf * 512:(hf + 1) * 512], ps[vbk[hf]][:, :], [bps[vbk[hf]]], [vo.b])
                cp('dve', vb5[:, hf * 4:(hf + 1) * 4, sub, :, 0:64],
                   ps[vbk[hf]][:, :].rearrange("p (pr e d) -> p pr e d", pr=4, e=2), [bps[vbk[hf]]], [vbt.b])
            if not sample:
                if KVF & 2:
                    dma('pool', vp[tok0:tok0 + 128, :], vo[:, :], [vo.b], [], vo.b.name)
            else:
                dma('pool', vso[:, :], vo[:, :], [vo.b], [], vo.b.name)
            if KVS < 5:
                continue
            tt('dve', zz[:, :], ps[4][:, 0:16], spar[:, SP_BF:SP_BF + 16], ALU.add, [bps[4], spar.b], [zz.b])
            act(ez[:, :], zz[:, :], AF.Exp, [zz.b], [ez.b], scale=-1.0)
            act(zz[:, :], ez[:, :], AF.Ln, [ez.b], [zz.b], bias=1.0)
            ts('dve', Lt[:, sub, :], zz[:, :], -1.0, ALU.mult, [zz.b], [Lt.b])
        if not sample:
            dma('pool', Vp[:, :, tile * 4:tile * 4 + 4, :].rearrange("pr p s c -> p pr s c"), vbt[:, :, :, :],
                [vbt.b], [bVp], vbt.b.name)
        else:
            for sq_ in range(2):
                dma('pool', Vs[sq_][:, 0:64, 32:33, :].rearrange("pr p s c -> p pr s c"),
                    vbt[sq_ * 64:(sq_ + 1) * 64, :, :, :], [vbt.b], [bVs[sq_]], vbt.b.name)
        if not sample:
            if KVF & 8:
                dma('pool', lfp[tile * 512:(tile + 1) * 512, :].rearrange("(s p) c -> p s c", p=128), Lt[:, 0:nsub, :],
                    [Lt.b], [], Lt.b.name)
            if KVF & 16:
                dma('pool', KTp[:, :, tile * 512:(tile + 1) * 512].rearrange("pr p t -> p pr t"), ktn[:, :, :],
                    [ktn.b], [bKTp], ktn.b.name)
            if KVF & 32:
                cumsum_tile(Lt, 4, carry_p, call_p, tile * 4, 4, pref)
        else:
            dma('pool', lfs[:, :], Lt[:, 0, :], [Lt.b], [], Lt.b.name)
            for sq_ in range(2):
                dma('pool', KTs[sq_][:, :, T:T + 64].rearrange("pr p t -> p pr t"),
                    ktn[:, :, sq_ * 64:(sq_ + 1) * 64], [ktn.b], [bKTs[sq_]], ktn.b.name)
            mm(ps[4][:, 32:48], cn32[:, C_TRIBLK, :], Lt[:, 0, :], True, True, [cn32.b, Lt.b], [bps[4]])
            cp('dve', pref[0:64, 0, :], carry_s[0][0:64, :], [carry_s[0].b], [pref.b])
            cp('dve', pref[64:128, 0, :], carry_s[1][64:128, :], [carry_s[1].b], [pref.b])
            tt('dve', pref[:, 1, :], ps[4][:, 32:48], pref[:, 0, :], ALU.add, [bps[4], pref.b], [pref.b])
            cp('dve', call_s[0][0:64, 32, :], pref[0:64, 1, :], [pref.b], [call_s[0].b])
            cp('dve', call_s[1][0:64, 32, :], pref[64:128, 1, :], [pref.b], [call_s[1].b])

    def fox(N, tile, sample):
        nsub = N // 128
        norm(N, SP_MIX + 8)
        arena_phase()
        qn = [aview("qn0", [1024], F32)]
        sqt = aview("sqt", [1024], F32)
        ssk = aview("ssk", [16], F32); lnk = aview("lnk", [16], F32); rk = aview("rk", [16], F32)
        QT = aview("QT", [16, 512], BF16)
        QT4 = QT.ap.rearrange("p (pr e) t -> p pr e t", e=2)
        memset('pool', QT[:, :, :], 0.0, [QT.b])
        sgate = aview("sgate", [8, 512], BF16)
        Aall = aview("Aall", [8, 512], BF16)
        Aall_bs = []
        for m_ in range(8):
            b_ = P.buf(f"Aall{m_}")
            b_.readers = list(Aall.b.readers)
            arena_state['bufs'].append(b_)
            Aall_bs.append(b_)
        NKMAX = T + 64 if sample else (tile + 1) * 512
        NKS = 33 if sample else (tile + 1) * 4
        ktp = [aview(f"ktp{k}", [NKMAX], BF16) for k in range(2)]
        vpr = [aview(f"vpr{k}", [NKS, 132], BF16) for k in range(2)]
        pt = [aview(f"pt{k}", [512], BF16) for k in range(6)]
        pt_rr = RR(range(6))
        LOOK = 3 if sample else 2
        Rb = aview("Rb", [4, 16], F32)
        HL = aview("HL", [512], BF16)
        cqv = aview("cqv", [512], F32, parts=16)
        rcol = aview("rcol", [8], F32, parts=16)
        memset('pool', HL[:, :], 0.0, [HL.b])
        rden = [aview(f"rden{k}", [512], F32) for k in range(2)]
        tO = [aview(f"tO{k}", [512], F32) for k in range(2)]
        wq = W8_INDEX['qg']
        pieces = [load8(wq + j) for j in range(2)]
        gate_state = {'piece': {}}

        def emit_gate_groups(sub):
            gps = 8 // nsub
            for gi in range(sub * gps, (sub + 1) * gps):
                pi, cc = gi // 4, gi % 4
                if pi not in gate_state['piece']:
                    gate_state['piece'][pi] = load8(wq + 2 + pi)
                s, v = gate_state['piece'][pi]
                b = psG.next()
                for k in range(8):
                    mm(ps[b][:, 0:N], v[:, k, cc * 128:(cc + 1) * 128], xn[:, k, 0:N], k == 0, k == 7,
                       [s.b, xn.bs[k]], [bps[b]])
                act(sgate[:, gi, 0:N], ps[b][:, 0:N], AF.Sigmoid, [bps[b]], [sgate.b])

        for sub in range(nsub):
            q_ = qn[0]
            qb = []
            for hf in range(2):
                b = psG.next()
                s, v = pieces[hf]
                for k in range(8):
                    mm(ps[b][:, :], xn[:, k, sub * 128:(sub + 1) * 128], v[:, k, :], k == 0, k == 7,
                       [s.b, xn.bs[k]], [bps[b]])
                qb.append(b)
            headnorm(qb[0], qb[1], SP_GQ, q_, sqt, ssk, lnk, rk)
            emit_gate_groups(sub)
            for half in range(2):
                b = psT.next()
                for i in range(4):
                    pr = half * 4 + i
                    tr(ps[b][:, i * 128:(i + 1) * 128], q_[:, pr * 128:(pr + 1) * 128], ident32, [q_.b, cn32.b], [bps[b]])
                pv = ps[b][:, :].rearrange("p (a t) -> p a t", t=128)
                act(QT4[0:64, half * 4:half * 4 + 4, 0, sub * 128:(sub + 1) * 128], pv[0:64], AF.Copy,
                    [bps[b], spar.b], [QT.b], scale=spar[0:64, SP_GQ2:SP_GQ2 + 1])
                ts('dve', QT4[64:128, half * 4:half * 4 + 4, 1, sub * 128:(sub + 1) * 128], pv[64:128],
                   spar[64:128, SP_GQ2:SP_GQ2 + 1], ALU.mult, [bps[b], spar.b], [QT.b])
        psS = RR([0, 1, 2, 6, 7]) if sample else RR([0, 1, 2])
        psO = RR([3, 4])
        if not sample:
            seqs = [dict(q0=0, nq=512, KT=KTp, bKT=bKTp, V=Vp, bV=bVp, call=call_p, nfull=tile * 4, ndiag=4,
                         sel=C_SEL127, crow0=0)]
        else:
            seqs = [dict(q0=sq_ * 64, nq=64, KT=KTs[sq_], bKT=bKTs[sq_], V=Vs[sq_], bV=bVs[sq_], call=call_s[sq_],
                         nfull=32, ndiag=1, sel=C_SEL63, crow0=0) for sq_ in range(2)]
        for sd in seqs:
            q0, nq, call = sd['q0'], sd['nq'], sd['call']
            nfull, ndiag = sd['nfull'], sd['ndiag']
            nks = nfull + ndiag
            qsub = min(128, nq)
            nj = nq // qsub
            kw = qsub
            nk = nfull * 128 + ndiag * kw
            mm(ps[5][:, 0:nj * 16], cn32[:, sd['sel'], :],
               call[:, nfull:nfull + nj, :].rearrange("p a b -> p (a b)"), True, True, [cn32.b, call.b], [bps[5]])
            cp('dve', Rb[:, 0:nj, :].rearrange("p a b -> p (a b)"), ps[5][:, 0:nj * 16], [bps[5]], [Rb.b])
            bias = aview(f"bias{q0}", [nks, 16], F32)
            tt('dve', bias[:, 0:nks, :], Rb[:, nj - 1, :].unsqueeze(1).broadcast_to([128, nks, 16]),
               call[:, 0:nks, :], ALU.subtract, [Rb.b, call.b], [bias.b])
            for j in range(nj):
                tr(ps[6][0:16, j * qsub:(j + 1) * qsub], call[0:qsub, nfull + j, :], ident32[0:qsub, 0:qsub],
                   [call.b, cn32.b], [bps[6]])
            cp('dve', rcol[0:16, 0:1], ps[6][0:16, nq - 1:nq], [bps[6]], [rcol.b])
            ts('dve', cqv[0:16, 0:nq], ps[6][0:16, 0:nq], rcol[0:16, 0:1], ALU.subtract, [bps[6], rcol.b], [cqv.b],
               s2=8.0, op1=ALU.mult)
            cp('dve', HL[0:16, q0:q0 + nq], cqv[0:16, 0:nq], [cqv.b], [HL.b])
            cp('dve', HL[64:80, q0:q0 + nq], cqv[0:16, 0:nq], [cqv.b], [HL.b])
            tt('dve', HL[32:48, q0:q0 + nq], cqv[0:16, 0:nq], HL[0:16, q0:q0 + nq], ALU.subtract, [cqv.b, HL.b], [HL.b])
            tt('dve', HL[96:112, q0:q0 + nq], cqv[0:16, 0:nq], HL[0:16, q0:q0 + nq], ALU.subtract, [cqv.b, HL.b], [HL.b])
            pending = []
            for pr in range(8):
                kt_ = ktp[pr % 2]; vp_ = vpr[pr % 2]
                dma('sp', kt_[:, 0:nk], sd['KT'][pr, :, 0:nk], [sd['bKT']], [kt_.b], kt_.b.name)
                dma('sp', vp_[:, 0:nks, :], sd['V'][pr, :, 0:nks, :], [sd['bV']], [vp_.b], vp_.b.name)
                for e_ in range(2):
                    hd = pr * 2 + e_
                    rows = slice(e_ * 64, (e_ + 1) * 64)
                    bo = psO.next()
                    for ks in range(nks):
                        diag = ks >= nfull
                        jd = ks - nfull
                        qs = jd * qsub if diag else 0
                        kn = kw if diag else 128
                        bs = psS.next()
                        mm(ps[bs][0:kn, qs:nq], kt_[:, ks * 128:ks * 128 + kn], QT[:, hd, q0 + qs:q0 + nq],
                           True, False, [kt_.b, QT.b], [bps[bs]])
                        mm(ps[bs][0:kn, qs:nq], sel2[:, hd, 0:kn], HL[:, q0 + qs:q0 + nq],
                           False, not diag, [sel2.b, HL.b], [bps[bs]])
                        if diag:
                            if kn == 128:
                                mm(ps[bs][0:kn, qs:qs + qsub], idbf[:, :], mnegbf[:, 0:qsub], False, True,
                                   [idbf.b, mnegbf.b], [bps[bs]])
                            else:
                                mm(ps[bs][0:kn, qs:qs + qsub], idbf[0:kn, 0:kn], mnegbf[0:kn, 0:qsub], False, True,
                                   [idbf.b, mnegbf.b], [bps[bs]])
                        p_ = pt[pt_rr.next()]
                        act(p_[0:kn, qs:nq], ps[bs][0:kn, qs:nq], AF.Exp, [bps[bs], bias.b], [p_.b],
                            bias=bias[0:kn, ks, hd:hd + 1], scale=0.125)
                        nxt = []
                        for cd_, f_ in pending:
                            if cd_ <= 0:
                                f_()
                            else:
                                nxt.append((cd_ - 1, f_))
                        pending = nxt
                        pending.append((LOOK - 1, (lambda bo=bo, vp_=vp_, p_=p_, kn=kn, ks=ks, e_=e_, qs=qs:
                                            mm(ps[bo][0:65, qs:nq], vp_[0:kn, ks, e_ * 66:e_ * 66 + 65], p_[0:kn, qs:nq],
                                               ks == 0, ks == nks - 1, [vp_.b, p_.b], [bps[bo]]))))

                    def epi_a(bo=bo, e_=e_):
                        rd = rden[e_]; to = tO[e_]
                        cp('act', to[0:65, 0:nq], ps[bo][0:65, 0:nq], [bps[bo]], [to.b])
                        P.op('dve', lambda e: e.reciprocal(out=rd[64:65, 0:nq], in_=to[64:65, 0:nq]),
                             reads=[to.b], writes=[rd.b])

                    def epi_b(bo=bo, e_=e_, pr=pr):
                        rd = rden[e_]; to = tO[e_]
                        mm(ps[5][0:64, 0:nq], cn32[64:65, C_ONES, 0:64], rd[64:65, 0:nq], True, True,
                           [cn32.b, rd.b], [bps[5]])
                        tt('dve', Aall[e_ * 64:(e_ + 1) * 64, pr, q0:q0 + nq], to[0:64, 0:nq], ps[5][0:64, 0:nq],
                           ALU.mult, [to.b, bps[5]], [Aall_bs[pr]])
                        if e_ == 1:
                            tt('pool', Aall[:, pr, q0:q0 + nq], Aall[:, pr, q0:q0 + nq], sgate[:, pr, q0:q0 + nq],
                               ALU.mult, [Aall_bs[pr], sgate.b], [Aall_bs[pr]])
                    pending.append((LOOK - 1, epi_a))
                    pending.append((min(LOOK + 7, 2 * nks - 2), epi_b))
            for cd_, f_ in pending:
                f_()
        for pi in range(2):
            s, v = load8(W8_INDEX['bwo'] + pi)
            bb4 = [psG.next() for cc in range(4)]
            if pi == 0:
                for k in range(8):
                    for cc in range(4):
                        mm(ps[bb4[cc]][:, 0:N], v[:, k, cc * 128:(cc + 1) * 128], Aall[:, k, 0:N], k == 0, k == 7,
                           [s.b, Aall_bs[k]], [bps[bb4[cc]]])
            for cc in range(4):
                c = pi * 4 + cc
                b = bb4[cc]
                if pi != 0:
                    for k in range(8):
                        mm(ps[b][:, 0:N], v[:, k, cc * 128:(cc + 1) * 128], Aall[:, k, 0:N], k == 0, k == 7,
                           [s.b, Aall_bs[k]], [bps[b]])
                tt('dve', h[:, c, 0:N], ps[b][:, 0:N], h[:, c, 0:N], ALU.add, [bps[b], h.b], [h.b])
                stat_acc(c, N)

    def convert_caches():
        arena_phase()
        kin = [aview(f"kin{k}", [4, 1024], F32) for k in range(2)]
        vins = [aview(f"vin{k}", [4, 1024], F32) for k in range(2)]
        ktcs = [aview(f"ktc{k}", [8, 512], BF16) for k in range(1)]
        vbcs = [aview(f"vbc{k}", [8, 4, 132], BF16) for k in range(2)]
        Lc = aview("Lc", [32, 16], F32)
        pref = aview("prefc", [32, 16], F32)
        for k in range(2):
            memset('pool', vbcs[k][:, :, :, :], 1.0, [vbcs[k].b])
        for sq_ in range(2):
            dma('pool', Lc[:, :, :], clf[sq_].rearrange("(s p) c -> p s c", p=128), [], [Lc.b], Lc.b.name)
            memset('dve', carry_s[sq_][:, :], 0.0, [carry_s[sq_].b])
            Lf = Lc[:, :, :].rearrange("p a b -> p (a b)")
            mm(ps[4][:, :], cn32[:, C_TRIFULL, :], Lf, True, True, [cn32.b, Lc.b], [bps[4]])
            mm(ps[7][:, :], cn32[:, C_ONES, :], Lf, True, True, [cn32.b, Lc.b], [bps[7]])
            cp('dve', pref[:, 0, :], carry_s[sq_][:, :], [carry_s[sq_].b], [pref.b])
            for i in range(1, 32):
                tt('dve', pref[:, i, :], pref[:, i - 1, :], ps[7][:, (i - 1) * 16:i * 16], ALU.add,
                   [pref.b, bps[7]], [pref.b])
            tt('dve', carry_s[sq_][:, :], pref[:, 31, :], ps[7][:, 31 * 16:32 * 16], ALU.add, [pref.b, bps[7]],
               [carry_s[sq_].b])
            tt('dve', call_s[sq_][:, 0:32, :], ps[4][:, :].rearrange("p (a b) -> p a b", b=16), pref[:, :, :], ALU.add,
               [bps[4], pref.b], [call_s[sq_].b])
            for kt in range(8):
                ki = kin[kt % 2]
                vin = vins[kt % 2]
                dma('sp', ki[:, :, :], ck[sq_, kt * 512:(kt + 1) * 512, :].rearrange("(s p) d -> p s d", p=128),
                    [], [ki.b], ki.b.name)
                dma('sp', vin[:, :, :], cv[sq_, kt * 512:(kt + 1) * 512, :].rearrange("(s p) d -> p s d", p=128),
                    [], [vin.b], vin.b.name)
                ktc = ktcs[0]; vbc = vbcs[kt % 2]
                vb5 = vbc.ap.rearrange("p pr s (e c) -> p pr s e c", e=2)
                for sub in range(4):
                    transposes_to(ki[:, sub, :], lambda pr0: ktc[:, pr0:pr0 + 4, sub * 128:(sub + 1) * 128],
                                  [ki.b], ktc.b)
                    cp(['act', 'dve', 'pool', 'dve'][sub], vb5[:, :, sub, :, 0:64],
                       vin[:, sub, :].rearrange("p (pr e d) -> p pr e d", pr=8, e=2), [vin.b], [vbc.b])
                dma('pool', KTs[sq_][:, :, kt * 512:(kt + 1) * 512].rearrange("pr p t -> p pr t"), ktc[:, :, :],
                    [ktc.b], [bKTs[sq_]], ktc.b.name)
                dma('pool', Vs[sq_][:, :, kt * 4:(kt + 1) * 4, :].rearrange("pr p s c -> p pr s c"), vbc[:, :, :, :],
                    [vbc.b], [bVs[sq_]], vbc.b.name)

    def x_load(tile, sample, xin):
        nsub = 1 if sample else 4
        src = xs if sample else xp[tile * 512:(tile + 1) * 512, :]
        dma('pool', xin[:, 0:nsub, :], src.rearrange("(s p) d -> p s d", p=128), [], [xin.b], xin.b.name)

    def x_prologue(sample, xin):
        N = 128 if sample else 512
        nsub = N // 128
        for c in range(8):
            b = psG.next()
            for sub in range(nsub):
                tr(ps[b][:, sub * 128:(sub + 1) * 128], xin[:, sub, c * 128:(c + 1) * 128], ident32,
                   [xin.b, cn32.b], [bps[b]])
            cp(evac_rr.next(), h[:, c, 0:N], ps[b][:, 0:N], [bps[b]], [h.b])
            stat_acc(c, N)

    def process_tile(tile, sample, prefetch_next=None):
        N = 128 if sample else 512
        nsub = N // 128
        dst = ys if sample else yp[tile * 512:(tile + 1) * 512, :]
        first = (tile == 0 and not sample)
        if sample:
            states = [(Sst[1], Sbf[1]), (Sst[2], Sbf[2])]
        else:
            states = [(Sst[0], Sbf[0])] * 8
        stages = [
            (lambda: (cast_ffn(0, 0, first=True), cast_group('win'), cast_group('awo')), lambda: ffn(N, 0, 0)),
            (lambda: cast_ffn(0, 1), lambda: gla(N, states)),
            (lambda: cast_group('kvf'), lambda: ffn(N, 0, 1)),
            (lambda: cast_ffn(1, 0), lambda: kv(N, tile, sample)),
            (lambda: (cast_group('qg'), cast_group('bwo')), lambda: ffn(N, 1, 0)),
            (lambda: cast_ffn(1, 1), lambda: fox(N, tile, sample)),
            (lambda: None, lambda: ffn(N, 1, 1)),
        ]
        wmode['first'] = first
        for si, (pre, run_) in enumerate(stages):
            if si >= (SSTOP if sample else STOP):
                break
            run_()
        wmode['first'] = False
        arena_phase()
        xin = aview("xout", [4, 1024], F32)
        xnext = None
        if prefetch_next is not None:
            xnext = aview("xinn", [4, 1024], F32)
            x_load(prefetch_next, False, xnext)
        for sub in range(nsub):
            for hf in range(2):
                b = psG.next()
                for i in range(4):
                    c = hf * 4 + i
                    tr(ps[b][:, i * 128:(i + 1) * 128], h[:, c, sub * 128:(sub + 1) * 128], ident32,
                       [h.b, cn32.b], [bps[b]])
                cp(evac_rr.next(), xin[:, sub, hf * 512:(hf + 1) * 512], ps[b][:, :], [bps[b]], [xin.b])
        dma('pool', dst.rearrange("(s p) d -> p s d", p=128), xin[:, 0:nsub, :], [xin.b], [], "xout")
        if xnext is not None:
            x_prologue(False, xnext)

    arena_phase()
    xin0 = aview("xin", [4, 1024], F32)
    x_load(0, False, xin0)
    x_prologue(False, xin0)
    for tile in range(NT):
        process_tile(tile, False, tile + 1 if tile + 1 < NT else None)
    if NT == 8:
        dma('pool', glap.rearrange("h d e -> d h e"), Sst[0][:, :].rearrange("p (h e) -> p h e", e=256),
            [Sst[0].b], [], "S0")
    if do_sample:
        for sq_ in range(2):
            dma('pool', Sst[1 + sq_][:, :].rearrange("p (h e) -> p h e", e=256), sgla[sq_].rearrange("h d e -> d h e"),
                [], [Sst[1 + sq_].b], f"S{1 + sq_}")
            cp('dve', Sbf[1 + sq_][:, :], Sst[1 + sq_][:, :], [Sst[1 + sq_].b], [Sbf[1 + sq_].b])
        if CONV:
            convert_caches()
        else:
            for sq_ in range(2):
                memset('dve', carry_s[sq_][:, :], 0.0, [carry_s[sq_].b])
        arena_phase()
        xin0 = aview("xin", [4, 1024], F32)
        x_load(0, True, xin0)
        x_prologue(True, xin0)
        process_tile(0, True)
        for sq_ in range(2):
            dma('pool', glas[sq_].rearrange("h d e -> d h e"), Sst[1 + sq_][:, :].rearrange("p (h e) -> p h e", e=256),
                [Sst[1 + sq_].b], [], f"S{1 + sq_}")
    stats = P.emit()
    if dbg:
        print('sbuf_bytes_remaining', nc.sbuf_bytes_remaining)
    return nc, stats


_CACHE = {}


def kernel(**inp):
    f = lambda a: np.ascontiguousarray(np.asarray(a, dtype=np.float32))
    shared = host_layout(inp)
    x_prompt = f(inp['x_prompt']); x_sample = f(inp['x_sample'])
    state_gla = f(inp['state_gla']); cache_k = f(inp['cache_k']); cache_v = f(inp['cache_v'])
    cache_logf = f(inp['cache_logf'])
    if 'nc' not in _CACHE:
        _CACHE['nc'] = build_program()[0]
    nc = _CACHE['nc']
    in_maps = []
    for c in range(8):
        m = dict(shared)
        m['xp'] = x_prompt[c]
        m['xs'] = x_sample[2 * c:2 * c + 2].reshape(128, D)
        m['sgla'] = state_gla[2 * c:2 * c + 2, 0]
        m['ck'] = cache_k[2 * c:2 * c + 2].reshape(2, T, D)
        m['cv'] = cache_v[2 * c:2 * c + 2].reshape(2, T, D)
        m['clf'] = cache_logf[2 * c:2 * c + 2]
        in_maps.append(m)
    res = run_bass_kernel_spmd(nc, in_maps, core_ids=list(range(8)))
    r = res.results
    g = lambda name: [np.asarray(r[c][name], dtype=np.float32) for c in range(8)]
    y_prompt = np.stack(g('yp'))
    y_sample = np.concatenate([a.reshape(2, 64, D) for a in g('ys')])
    gla_prompt = np.stack(g('glap'))[:, None]
    gla_sample = np.concatenate(g('glas'))[:, None]
    k_prompt = np.stack(g('kp')).reshape(8, T, 16, 64)
    v_prompt = np.stack(g('vp')).reshape(8, T, 16, 64)
    lf_prompt = np.stack(g('lfp'))
    k_sample = np.concatenate([a.reshape(2, 64, 16, 64) for a in g('kso')])
    v_sample = np.concatenate([a.reshape(2, 64, 16, 64) for a in g('vso')])
    lf_sample = np.concatenate([a.reshape(2, 64, 16) for a in g('lfs')])
    return (y_prompt, y_sample, gla_prompt, gla_sample, k_prompt, v_prompt, lf_prompt, k_sample, v_sample, lf_sample)
```
